# Optimizing a Trainium2 kernel written in Bass

```python
import jax, jax.numpy as jnp
from jax import lax
import numpy as np

D_MODEL = 2048
BATCH = 2
SEQ = 16384
DEPTH = 2
DEC_BATCH = 32
DEC_SEQ = 16
PAST_LEN = 4096

CHUNK = 64
N_BRANCH = 4
BRANCH_WIDTH = D_MODEL // N_BRANCH
HEAD_DIM = 128
N_HEADS = BRANCH_WIDTH // HEAD_DIM
CONV_WIDTH = 31
CONV_CH = BRANCH_WIDTH
Q_BLOCK = 128
ROPE_BASE = 10000.0
NORM_EPS = 1e-6
LN_EPS = 1e-5
N_IN = 16 * BRANCH_WIDTH + 2 * N_HEADS + N_BRANCH * D_MODEL

kernel_name = 'streaming_hybrid_gated_encoder'


def _split_cols(p):
    W, H, D = BRANCH_WIDTH, N_HEADS, D_MODEL
    sizes = [W] * 4 + [W] * 5 + [H, H] + [W] * 4 + [W] * 3 + [D] * N_BRANCH
    idx = np.cumsum(sizes)[:-1].tolist()
    return jnp.split(p, idx, axis=-1)


def _rmsnorm(x, g):
    x32 = x.astype(jnp.float32)
    y = x32 * lax.rsqrt(jnp.mean(x32 * x32, axis=-1, keepdims=True) + NORM_EPS)
    return (y * g.astype(jnp.float32)).astype(x.dtype)


def _head_norm(x):
    mu = jnp.mean(x, axis=-1, keepdims=True)
    xc = x - mu
    return xc * lax.rsqrt(jnp.mean(xc * xc, axis=-1, keepdims=True) + LN_EPS)


def _layernorm(x, g, b):
    x32 = x.astype(jnp.float32)
    return _head_norm(x32) * g.astype(jnp.float32) + b.astype(jnp.float32)


def _rotary(x, pos):
    half = HEAD_DIM // 2
    inv = ROPE_BASE ** (-jnp.arange(half, dtype=jnp.float32) / half)
    ang = pos.astype(jnp.float32)[:, None] * inv[None, :]
    cos = jnp.cos(ang)[None, :, None, :]
    sin = jnp.sin(ang)[None, :, None, :]
    x1, x2 = x[..., :half], x[..., half:]
    return jnp.concatenate([x1 * cos - x2 * sin, x1 * sin + x2 * cos], axis=-1)


def _sb_block(q, k, v, q_pos, k_pos):
    z = jnp.einsum('bqhd,bkhd->bhqk', q, k).astype(jnp.float32) * (HEAD_DIM ** -0.5)
    causal = (k_pos[None, :] < q_pos[:, None])[None, None]
    log_1mb = jnp.where(causal, jax.nn.log_sigmoid(-z), 0.0)
    later = lax.cumsum(log_1mb, axis=3, reverse=True) - log_1mb
    w = jnp.where(causal, jnp.exp(jax.nn.log_sigmoid(z) + later), 0.0)
    return jnp.einsum('bhqk,bkhd->bqhd', w.astype(v.dtype), v)


def _stick_breaking(q, k, v, q_pos, k_pos):
    B, T, H, d = q.shape
    blk = min(Q_BLOCK, T)
    nb = T // blk
    qb = q.reshape(B, nb, blk, H, d).transpose(1, 0, 2, 3, 4)
    pb = q_pos.reshape(nb, blk)
    out = lax.map(lambda a: _sb_block(a[0], k, v, a[1], k_pos), (qb, pb))
    return out.transpose(1, 0, 2, 3, 4).reshape(B, T, H, d)


def _mlstm(q, k, v, i_pre, log_f, C0, n0, m0):
    B, T, H, d = q.shape
    L = min(CHUNK, T)
    nc = T // L
    to_chunks = lambda a: a.reshape((B, nc, L) + a.shape[2:]).swapaxes(0, 1)
    tri = jnp.tril(jnp.ones((L, L), bool))

    def step(carry, xs):
        C, n, m = carry
        qc, kc, vc, ic, fc = xs
        b = jnp.cumsum(fc, axis=1).swapaxes(1, 2)
        iT = ic.swapaxes(1, 2)
        logD = jnp.where(tri, b[..., :, None] - b[..., None, :] + iT[..., None, :], -jnp.inf)
        inter = m[..., None] + b
        m_row = jnp.maximum(inter, jnp.max(logD, axis=-1))
        w = jnp.exp(logD - m_row[..., None]) * jnp.einsum('bthd,bshd->bhts', qc, kc)
        a_inter = jnp.exp(inter - m_row)
        num = (jnp.einsum('bhts,bshe->bthe', w, vc)
               + a_inter.swapaxes(1, 2)[..., None] * jnp.einsum('bthd,bhde->bthe', qc, C))
        den = jnp.sum(w, axis=-1) + a_inter * jnp.einsum('bthd,bhd->bht', qc, n)
        h = num / jnp.maximum(jnp.abs(den), jnp.exp(-m_row)).swapaxes(1, 2)[..., None]
        g = b[..., -1:] - b + iT
        m_new = jnp.maximum(m + b[..., -1], jnp.max(g, axis=-1))
        decay = jnp.exp(m + b[..., -1] - m_new)
        wg = jnp.exp(g - m_new[..., None])
        C_new = decay[..., None, None] * C + jnp.einsum('bhs,bshd,bshe->bhde', wg, kc, vc)
        n_new = decay[..., None] * n + jnp.einsum('bhs,bshd->bhd', wg, kc)
        return (C_new, n_new, m_new), h

    (C, n, m), hs = lax.scan(step, (C0, n0, m0), tuple(to_chunks(a) for a in (q, k, v, i_pre, log_f)))
    return hs.swapaxes(0, 1).reshape(B, T, H, d), C, n, m


def _retention(q, k, v, S0):
    B, T, H, d = q.shape
    L = min(CHUNK, T)
    nc = T // L
    lg = jnp.log1p(-jnp.exp2(-5.0 - jnp.arange(H, dtype=jnp.float32)))
    t = jnp.arange(L, dtype=jnp.float32)
    rel = t[:, None] - t[None, :]
    decay_mask = jnp.where(rel >= 0, jnp.exp(lg[:, None, None] * jnp.maximum(rel, 0.0)), 0.0)
    q_decay = jnp.exp(lg[:, None] * (t + 1.0)).T[None, :, :, None]
    k_decay = jnp.exp(lg[:, None] * (L - 1.0 - t))
    chunk_decay = jnp.exp(lg * L)[:, None, None]
    to_chunks = lambda a: a.reshape(B, nc, L, H, d).swapaxes(0, 1)

    def step(S, xs):
        qc, kc, vc = xs
        att = jnp.einsum('bthd,bshd->bhts', qc, kc) * decay_mask
        o = jnp.einsum('bhts,bshe->bthe', att, vc) + jnp.einsum('bthd,bhde->bthe', qc, S) * q_decay
        S_new = chunk_decay * S + jnp.einsum('bshd,bshe,hs->bhde', kc, vc, k_decay)
        return S_new, o

    S, os_ = lax.scan(step, S0, (to_chunks(q), to_chunks(k), to_chunks(v)))
    return os_.swapaxes(0, 1).reshape(B, T, H, d), S


def _causal_dwconv(u, buf, w, b):
    xp = jnp.concatenate([buf.astype(u.dtype), u], axis=1)
    y = lax.conv_general_dilated(xp, w[:, None, :].astype(u.dtype), window_strides=(1,), padding='VALID',
                                 dimension_numbers=('NWC', 'WIO', 'NWC'), feature_group_count=u.shape[-1])
    return y + b.astype(u.dtype), xp[:, xp.shape[1] - (CONV_WIDTH - 1):]


def _layer(x, pos, past_kv, C0, n0, m0, S0, conv0,
           norm_g, w_in, b_i, b_f, conv_w, conv_b, ln_g, ln_b, w_branch, w_out):
    B, T, _ = x.shape
    f32 = jnp.float32
    dt = x.dtype
    hin = _rmsnorm(x, norm_g)
    (qa, ka, va, za, qb, kb, vb, ob, zb, ib, fb,
     qc, kc, vc, zc, ud, gd, zd, *gate_pre) = _split_cols(hin @ w_in)
    heads = lambda a: a.reshape(B, T, N_HEADS, HEAD_DIM)

    ka_h, va_h = heads(ka), heads(va)
    if past_kv is None:
        k_all, v_all = ka_h, va_h
    else:
        k_all = jnp.concatenate([past_kv[0].astype(dt), ka_h], axis=1)
        v_all = jnp.concatenate([past_kv[1].astype(dt), va_h], axis=1)
    k_pos = jnp.arange(k_all.shape[1])
    ya = _stick_breaking(heads(qa), k_all, v_all, pos, k_pos).reshape(B, T, BRANCH_WIDTH) * jax.nn.silu(za)

    hb, C1, n1, m1 = _mlstm(heads(qb).astype(f32), heads(kb).astype(f32) * (HEAD_DIM ** -0.5),
                            heads(vb).astype(f32), ib.astype(f32) + b_i.astype(f32),
                            jax.nn.log_sigmoid(fb.astype(f32) + b_f.astype(f32)),
                            C0.astype(f32), n0.astype(f32), m0.astype(f32))
    yb = (_head_norm(hb).reshape(B, T, BRANCH_WIDTH) * jax.nn.sigmoid(ob.astype(f32))).astype(dt) * jax.nn.silu(zb)

    qr = _rotary(heads(qc).astype(f32), pos)
    kr = _rotary(heads(kc).astype(f32), pos) * (HEAD_DIM ** -0.5)
    hc, S1 = _retention(qr, kr, heads(vc).astype(f32), S0.astype(f32))
    yc = _head_norm(hc).reshape(B, T, BRANCH_WIDTH).astype(dt) * jax.nn.silu(zc)

    glu = ud * jax.nn.sigmoid(gd)
    cv, conv1 = _causal_dwconv(glu, conv0, conv_w, conv_b)
    yd = jax.nn.silu(_layernorm(cv, ln_g, ln_b)).astype(dt) * jax.nn.silu(zd)

    merged = jax.nn.sigmoid(gate_pre[0]) * (ya @ w_branch[0])
    merged = merged + jax.nn.sigmoid(gate_pre[1]) * (yb @ w_branch[1])
    merged = merged + jax.nn.sigmoid(gate_pre[2]) * (yc @ w_branch[2])
    merged = merged + jax.nn.sigmoid(gate_pre[3]) * (yd @ w_branch[3])
    new_state = (ka_h, va_h, C1.astype(dt), n1.astype(dt), m1.astype(dt), S1.astype(dt), conv1)
    return x + merged @ w_out, new_state


def _trunk(x, past_len, cache_k, cache_v, C0, n0, m0, S0, conv0,
           norm_g, w_in, b_i, b_f, conv_w, conv_b, ln_g, ln_b, w_branch, w_out, final_g):
    T = x.shape[1]
    pos = past_len + jnp.arange(T)
    per_layer = []
    for l in range(DEPTH):
        past = None if cache_k is None else (cache_k[l], cache_v[l])
        x, st = _layer(x, pos, past, C0[l], n0[l], m0[l], S0[l], conv0[l],
                       norm_g[l], w_in[l], b_i[l], b_f[l], conv_w[l], conv_b[l],
                       ln_g[l], ln_b[l], w_branch[l], w_out[l])
        per_layer.append(st)
    stacked = tuple(jnp.stack(s, axis=0) for s in zip(*per_layer))
    return _rmsnorm(x, final_g), stacked


def setup_inputs(seed: int = 0) -> dict:
    key = jax.random.key(seed)
    ks = jax.random.split(key, 20)
    H, HD, W, D = N_HEADS, HEAD_DIM, BRANCH_WIDTH, D_MODEL
    nrm = lambda k, shape, s=1.0: s * jax.random.normal(k, shape, jnp.float32)
    return {
        'x_prompt': nrm(ks[0], (BATCH, SEQ, D)),
        'x_sample': nrm(ks[1], (DEC_BATCH, DEC_SEQ, D)),
        'cache_sb_k': nrm(ks[2], (DEPTH, DEC_BATCH, PAST_LEN, H, HD)),
        'cache_sb_v': nrm(ks[3], (DEPTH, DEC_BATCH, PAST_LEN, H, HD)),
        'state_mlstm_C': nrm(ks[4], (DEPTH, DEC_BATCH, H, HD, HD), 0.5),
        'state_mlstm_n': nrm(ks[5], (DEPTH, DEC_BATCH, H, HD), 0.5),
        'state_mlstm_m': nrm(ks[6], (DEPTH, DEC_BATCH, H)),
        'state_ret_S': nrm(ks[7], (DEPTH, DEC_BATCH, H, HD, HD)),
        'state_conv': nrm(ks[8], (DEPTH, DEC_BATCH, CONV_WIDTH - 1, CONV_CH), 0.5),
        'norm_g': 1.0 + nrm(ks[9], (DEPTH, D), 0.02),
        'w_in': nrm(ks[10], (DEPTH, D, N_IN), D ** -0.5),
        'mlstm_b_i': nrm(ks[11], (DEPTH, H), 0.1),
        'mlstm_b_f': jnp.linspace(3.0, 6.0, H, dtype=jnp.float32)[None, :] + nrm(ks[12], (DEPTH, H), 0.1),
        'conv_w': nrm(ks[13], (DEPTH, CONV_WIDTH, CONV_CH), CONV_WIDTH ** -0.5),
        'conv_b': nrm(ks[14], (DEPTH, CONV_CH), 0.02),
        'conv_ln_g': 1.0 + nrm(ks[15], (DEPTH, CONV_CH), 0.02),
        'conv_ln_b': nrm(ks[16], (DEPTH, CONV_CH), 0.02),
        'w_branch': nrm(ks[17], (DEPTH, N_BRANCH, W, D), W ** -0.5),
        'w_out': nrm(ks[18], (DEPTH, D, D), D ** -0.5),
        'final_g': 1.0 + nrm(ks[19], (D,), 0.02),
    }


def reference(x_prompt, x_sample, cache_sb_k, cache_sb_v, state_mlstm_C, state_mlstm_n, state_mlstm_m,
              state_ret_S, state_conv, norm_g, w_in, mlstm_b_i, mlstm_b_f, conv_w, conv_b,
              conv_ln_g, conv_ln_b, w_branch, w_out, final_g):
    Bp = x_prompt.shape[0]
    zC = jnp.zeros((DEPTH, Bp, N_HEADS, HEAD_DIM, HEAD_DIM), jnp.float32)
    zn = jnp.zeros((DEPTH, Bp, N_HEADS, HEAD_DIM), jnp.float32)
    zm = jnp.zeros((DEPTH, Bp, N_HEADS), jnp.float32)
    zconv = jnp.zeros((DEPTH, Bp, CONV_WIDTH - 1, CONV_CH), x_prompt.dtype)
    y_prompt, (pk, pv, pC, pn, pm, pS, pconv) = _trunk(
        x_prompt, 0, None, None, zC, zn, zm, zC, zconv,
        norm_g, w_in, mlstm_b_i, mlstm_b_f, conv_w, conv_b, conv_ln_g, conv_ln_b, w_branch, w_out, final_g)
    y_sample, (sk, sv, sC, sn, sm, sS, sconv) = _trunk(
        x_sample, cache_sb_k.shape[2], cache_sb_k, cache_sb_v, state_mlstm_C, state_mlstm_n, state_mlstm_m,
        state_ret_S, state_conv,
        norm_g, w_in, mlstm_b_i, mlstm_b_f, conv_w, conv_b, conv_ln_g, conv_ln_b, w_branch, w_out, final_g)
    return (y_prompt, y_sample, pk, pv, pC, pn, pm, pS, pconv, sk, sv, sC, sn, sm, sS, sconv)
```

```python
import os
import numpy as np
import ml_dtypes
from contextlib import ExitStack
import concourse.bass as bass
import concourse.mybir as mybir
from concourse.bass_utils import run_bass_kernel_spmd

F32 = mybir.dt.float32
BF = mybir.dt.bfloat16
AF = mybir.ActivationFunctionType
ALU = mybir.AluOpType

D = 2048
T = 16384
DEPTH = 2
H = 4
HD = 128
NIN = 16392
NS = 4
TS = 16
PAST = 4096
CW = 31
NDC = 16
EPS = 1e-6
LNEPS = 1e-5
SC = HD ** -0.5
OA = 0
OB = 2048
OIF = 4608
OC = 4616
OD = 6664
OG = 8200

NSB_RUN = int(os.environ.get("MK_NSB", "32"))
DO_SAMPLE = int(os.environ.get("MK_SAMPLE", "1"))
NLAYER = int(os.environ.get("MK_LAYERS", "2"))


class Sched:
    ENG = ['pe', 'act', 'dve', 'pool', 'sp']
    NDMA = 24
    NQ = 8

    def __init__(self, nc, stack):
        self.nc = nc
        self.prog = {e: [] for e in self.ENG}
        self.cnt = {e: 0 for e in self.ENG}
        self.seen = {e: {} for e in self.ENG}
        self.last_w = {}
        self.readers = {}
        self.sem = {}
        for e in ['pe', 'act', 'dve', 'pool']:
            self.sem[e] = stack.enter_context(nc.semaphore('s_' + e))
        self.dcount = {}
        for i in range(self.NDMA):
            k = 'd%d' % i
            self.sem[k] = stack.enter_context(nc.semaphore('s_' + k))
            self.dcount[k] = 0
        self.dnext = 0
        self.qnext = 0
        for i in range(self.NQ):
            k = 'q%d' % i
            self.sem[k] = stack.enter_context(nc.semaphore('s_' + k))
            self.dcount[k] = 0
        self.nops = 0

    def _deps(self, eng, reads, writes):
        deps = set()
        for b in reads:
            if b in self.last_w:
                deps.add(self.last_w[b])
        for b in writes:
            if b in self.last_w:
                deps.add(self.last_w[b])
            for kv in self.readers.get(b, {}).items():
                deps.add(kv)
        for (k, v) in sorted(deps):
            if k == eng and eng == 'pe':
                continue
            if self.seen[eng].get(k, 0) < v:
                self.prog[eng].append(('wait', k, v))
                self.seen[eng][k] = v

    def _commit(self, me, reads, writes):
        for b in writes:
            self.last_w[b] = me
            self.readers[b] = {}
        for b in reads:
            self.readers.setdefault(b, {})[me[0]] = me[1]

    def op(self, eng, fn, reads=(), writes=()):
        self._deps(eng, reads, writes)
        self.cnt[eng] += 1
        self.prog[eng].append(('op', fn))
        self._commit((eng, self.cnt[eng]), reads, writes)
        self.nops += 1

    def dma(self, q, out, in_, reads=(), writes=()):
        if q == 'pool':
            d = 'q%d' % self.qnext
            self.qnext = (self.qnext + 1) % self.NQ
        else:
            d = 'd%d' % self.dnext
            self.dnext = (self.dnext + 1) % self.NDMA
        if self.dcount[d] > 0 and self.seen[q].get(d, 0) < 16 * self.dcount[d]:
            self.prog[q].append(('wait', d, 16 * self.dcount[d]))
            self.seen[q][d] = 16 * self.dcount[d]
        self._deps(q, reads, writes)
        self.dcount[d] += 1
        self.prog[q].append(('dma', out, in_, d))
        self._commit((d, 16 * self.dcount[d]), reads, writes)
        self.nops += 1

    def finish(self):
        for d, c in self.dcount.items():
            if c > 0 and self.seen['sp'].get(d, 0) < 16 * c:
                self.prog['sp'].append(('wait', d, 16 * c))
        for e in ['pe', 'act', 'dve', 'pool']:
            if self.cnt[e] > 0:
                self.prog['sp'].append(('wait', e, self.cnt[e]))

    def emit(self):
        nc = self.nc
        sem = self.sem

        def replay(name, e):
            for it in self.prog[name]:
                if it[0] == 'wait':
                    e.wait_ge(sem[it[1]], it[2])
                elif it[0] == 'op':
                    it[1](e).then_inc(sem[name], 1)
                elif it[0] == 'dma':
                    e.dma_start(out=it[1], in_=it[2]).then_inc(sem[it[3]], 16)

        with nc.Block() as block:
            @block.tensor
            def _(e):
                replay('pe', e)

            @block.scalar
            def _(e):
                replay('act', e)

            @block.vector
            def _(e):
                replay('dve', e)

            @block.gpsimd
            def _(e):
                replay('pool', e)

            @block.sync
            def _(e):
                replay('sp', e)


def _gammas():
    lg = np.log1p(-np.exp2(-5.0 - np.arange(H, dtype=np.float64)))
    return lg


CF_LAYOUT = {}
CB_LAYOUT = {}


def _build_consts():
    cf = []
    cb = []

    def addf(name, a):
        a = np.asarray(a, np.float32)
        CF_LAYOUT[name] = (sum(x.shape[1] for x in cf), a.shape[1])
        cf.append(a)

    def addb(name, a):
        a = np.asarray(a, np.float32)
        CB_LAYOUT[name] = (sum(x.shape[1] for x in cb), a.shape[1])
        cb.append(a)

    p = np.arange(128)[:, None]
    f = np.arange(128)[None, :]
    addf('ident', (p == f))
    addf('ones', np.ones((128, 128)))
    addf('zeros', np.zeros((128, 128)))
    addf('negtri', -(p <= f).astype(np.float32))
    addf('negones', -np.ones((128, 128)))
    addf('masksc', (p <= f) * SC)
    addf('mask01', (p <= f))
    addf('o512', np.full((128, 128), 1.0 / 512.0))
    lg = _gammas()
    t1 = np.arange(128, dtype=np.float64)[:, None] + 1.0
    addf('qdec', np.exp(lg[None, :] * t1))
    addf('kdec', np.exp(-lg[None, :] * t1) * SC)
    q = np.arange(512)[None, :]
    for j in range(4):
        valid = (p + 128 * j) < q
        addb('negm%d' % j, np.where(valid, 0.0, -30000.0))
        addb('m01_%d' % j, valid)
    addb('identb', (p == f))
    addb('onesb', np.ones((128, 128)))
    addb('neguincl', -(p >= f).astype(np.float32))
    addb('negonesb', -np.ones((128, 128)))
    cfa = np.concatenate(cf, 1).astype(np.float32)
    cba = np.concatenate(cb, 1).astype(ml_dtypes.bfloat16)
    return cfa, cba


def _rot_tables():
    half = HD // 2
    inv = (np.float32(10000.0) ** (-np.arange(half, dtype=np.float32) / np.float32(half))).astype(np.float32)
    pos = np.concatenate([np.arange(T), PAST + np.arange(TS)]).astype(np.float32)
    ang = (pos[:, None] * inv[None, :]).astype(np.float32)
    return np.cos(ang).astype(np.float32), np.sin(ang).astype(np.float32)


def build_program():
    nc = bass.Bass("TRN2", target_bir_lowering=False)
    cfa, cba = _build_consts()
    NCF, NCB = cfa.shape[1], cba.shape[1]

    def din(name, shape, dt=F32):
        return nc.dram_tensor(name, list(shape), dt, kind="ExternalInput").ap()

    def dout(name, shape, dt=F32):
        return nc.dram_tensor(name, list(shape), dt, kind="ExternalOutput").ap()

    def dscr(name, shape, dt):
        return nc.dram_tensor(name, list(shape), dt).ap()

    I = dict(
        xT=din('xT', [D, T]), xsT=din('xsT', [NS, D, TS]),
        w_in=din('w_in', [DEPTH, D, NIN]), w_br=din('w_br', [DEPTH, 4, 512, D]), w_out=din('w_out', [DEPTH, D, D]),
        gcol=din('gcol', [128, DEPTH * 16]), fgcol=din('fgcol', [128, 16]),
        bif=din('bif', [128, DEPTH * 8]),
        convw=din('convw', [128, DEPTH * 4 * CW]), convb=din('convb', [128, DEPTH * 4]),
        lng=din('lng', [128, DEPTH * 4]), lnb=din('lnb', [128, DEPTH * 4]),
        cache_k=din('cache_k', [DEPTH, NS, PAST, 512]), cache_v=din('cache_v', [DEPTH, NS, PAST, 512]),
        stC=din('stC', [DEPTH, NS, H, 128, 128]), stn=din('stn', [128, DEPTH * NS * H]),
        stm=din('stm', [128, DEPTH * NS * H]), stS=din('stS', [DEPTH, NS, H, 128, 128]),
        stconv=din('stconv', [DEPTH, NS, 4, 128, CW - 1]),
        rcos=din('rcos', [T + TS, 64]), rsin=din('rsin', [T + TS, 64]),
        cf=din('cf', [128, NCF]), cb=din('cb', [128, NCB], BF),
    )
    O = dict(
        yT=dout('yT', [D, T]), ysT=dout('ysT', [NS, D, TS]),
        kT=dout('kT_o', [DEPTH, H, 128, T]), v=dout('v_o', [DEPTH, T, 512]),
        ksT=dout('ksT_o', [DEPTH, NS, H, 128, TS]), vs=dout('vs_o', [DEPTH, NS, TS, 512]),
        Cn=dout('Cn_o', [DEPTH, 1 + NS, H, 128, 129]), m=dout('m_o', [DEPTH, 1 + NS, H, 128, 1]),
        S=dout('S_o', [DEPTH, 1 + NS, H, 128, 128]), conv=dout('conv_o', [DEPTH, 1 + NS, 4, 128, CW - 1]),
    )
    X1 = dout('x1T', [D, T], F32) if NLAYER == 1 else dscr('x1T', [D, T], F32)
    X1s = dscr('x1sT', [NS, D, TS], F32)
    KSC = dscr('kscr', [DEPTH, H, 128, T], BF)
    VSC = dscr('vscr', [DEPTH, T, 512], BF)
    KSS = dscr('kscr_s', [DEPTH, NS, H, 128, PAST + TS], BF)
    VSS = dscr('vscr_s', [DEPTH, NS, PAST + TS, 512], BF)

    with ExitStack() as st:
        S = Sched(nc, st)

        def sb(name, shape, dt):
            return st.enter_context(nc.sbuf_tensor('sb_' + name, list(shape), dt))

        def pst(name, shape, dt):
            return st.enter_context(nc.psum_tensor('pp_' + name, list(shape), dt))

        def act(out, in_, func, r, w, scale=1.0, bias=None):
            if bias is None:
                S.op('act', lambda e: e.activation(out=out, in_=in_, func=func, scale=scale), r, w)
            else:
                S.op('act', lambda e: e.activation(out=out, in_=in_, func=func, scale=scale, bias=bias), r, w)

        def tt(out, a, b, op, r, w, eng='dve'):
            S.op(eng, lambda e: e.tensor_tensor(out=out, in0=a, in1=b, op=op), r, w)

        def ts(out, a, s1, op0, r, w, s2=None, op1=None, eng='dve'):
            if op1 is None:
                S.op(eng, lambda e: e.tensor_scalar(out=out, in0=a, scalar1=s1, scalar2=None, op0=op0), r, w)
            else:
                S.op(eng, lambda e: e.tensor_scalar(out=out, in0=a, scalar1=s1, scalar2=s2, op0=op0, op1=op1), r, w)

        def stt(out, a, s, b, op0, op1, r, w):
            S.op('dve', lambda e: e.scalar_tensor_tensor(out=out, in0=a, scalar=s, in1=b, op0=op0, op1=op1), r, w)

        def cp(out, in_, r, w, eng='dve'):
            S.op(eng, lambda e: e.tensor_copy(out=out, in_=in_), r, w)

        def mm(out, lhsT, rhs, start, stop, r, w):
            S.op('pe', lambda e: e.matmul(out, lhsT=lhsT, rhs=rhs, start=start, stop=stop), r, w)

        def tr(out, in_, ident, r, w):
            S.op('pe', lambda e: e.transpose(out, in_, ident), r, w)

        def memset(ap, val, w, eng='pool'):
            S.op(eng, lambda e: e.memset(ap, val), (), w)

        cf = sb('cf', [128, NCF], F32)
        cbt = sb('cbt', [128, NCB], BF)
        S.dma('sp', cf[:], I['cf'], (), ['cf'])
        S.dma('sp', cbt[:], I['cb'], (), ['cb'])

        def CF(name, rows=128, c0=0, c1=None):
            o, n = CF_LAYOUT[name]
            c1 = n if c1 is None else c1
            return cf[0:rows, o + c0:o + c1]

        def CB(name, rows=128, c0=0, c1=None):
            o, n = CB_LAYOUT[name]
            c1 = n if c1 is None else c1
            return cbt[0:rows, o + c0:o + c1]

        gcol = sb('gcol', [128, DEPTH * 16], F32)
        fgcol = sb('fgcol', [128, 16], F32)
        bif = sb('bif', [128, DEPTH * 8], F32)
        nbf = sb('nbf', [128, DEPTH * 8], F32)
        convw = sb('convw', [128, DEPTH * 4 * CW], F32)
        convb = sb('convb', [128, DEPTH * 4], F32)
        lng = sb('lng', [128, DEPTH * 4], F32)
        lnb = sb('lnb', [128, DEPTH * 4], F32)
        stn = sb('stn', [128, DEPTH * NS * H], F32)
        stm = sb('stm', [128, DEPTH * NS * H], F32)
        for nm, t_ in [('gcol', gcol), ('fgcol', fgcol), ('bif', bif), ('convw', convw), ('convb', convb),
                       ('lng', lng), ('lnb', lnb), ('stn', stn), ('stm', stm)]:
            S.dma('sp', t_[:], I[nm], (), ['par'])
        ts(nbf[:], bif[:], -1.0, ALU.mult, ['par'], ['par2'])

        NMAX = 512
        xs = [sb('xs%d' % i, [128, NMAX], F32) for i in range(2)]
        sqb = [sb('sqb%d' % i, [128, NMAX], BF) for i in range(2)]
        xb = sb('xb', [128, NDC, NMAX], BF)
        rstd_bc = sb('rstd_bc', [128, NMAX], F32)
        rstd_col = sb('rstd_col', [128, 4], F32)
        wb = [sb('wb%d' % i, [128, NDC, 512], BF) for i in range(2)]
        wsm = sb('wsm', [128, NDC, 8], BF)
        wbr = [sb('wbr%d' % i, [128, 4, 512], BF) for i in range(2)]
        merged = sb('merged', [128, NDC, NMAX], F32)
        mbf = xb
        yT = sb('yT', [128, 4, NMAX], BF)
        fmA = sb('fmA', [128, 4, NMAX], BF)
        fmB = sb('fmB', [128, 4, NMAX], BF)
        gate = sb('gate', [128, 4, NMAX], BF)
        f32t = [sb('f32t%d' % i, [128, NMAX], F32) for i in range(2)]
        tmA = sb('tmA', [128, 4, 4, 130], BF)
        tmB = sb('tmB', [128, 4, 512], BF)
        tmC = sb('tmC', [128, 4, 512], BF)
        tmF = [sb('tmF%d' % i, [128, 512], F32) for i in range(2)]
        ifc = sb('ifc', [128, 4, 8], F32)
        sig = [sb('sig%d' % i, [128, NMAX], F32) for i in range(2)]
        kblk = [sb('kblk%d' % i, [128, 512], BF) for i in range(2)]
        vblk = [sb('vblk%d' % i, [128, 4, 128], BF) for i in range(2)]
        a_e = [sb('a_e%d' % i, [128, NMAX], F32) for i in range(2)]
        a_sp = [sb('a_sp%d' % i, [128, NMAX], BF) for i in range(2)]
        a_x = [sb('a_x%d' % i, [128, NMAX], F32) for i in range(2)]
        a_w = [sb('a_w%d' % i, [128, NMAX], BF) for i in range(2)]
        carry = [sb('carry%d' % i, [128, NMAX], F32) for i in range(2)]
        Cext = [sb('Cext%d' % h, [128, 129], F32) for h in range(H)]
        Cbf = [sb('Cbf%d' % h, [128, 130], BF) for h in range(H)]
        Sst = [sb('Sst%d' % h, [128, 128], F32) for h in range(H)]
        Sbf = [sb('Sbf%d' % h, [128, 128], BF) for h in range(H)]
        Fc = sb('Fc', [128, H], F32)
        Mcc = sb('Mcc', [128, H], F32)
        Mcr = sb('Mcr', [1, H], F32)
        sm = sb('sm', [128, 64], F32)
        Mrow = sb('Mrow', [1, 128], F32)
        Dt = sb('Dt', [128, 128], F32)
        Wt = sb('Wt', [128, 128], BF)
        nd = sb('nd', [128, 130], F32)
        hh = sb('hh', [128, 128], F32)
        hn = sb('hn', [128, 128], BF)
        kw = sb('kw', [128, 128], BF)
        bnst = sb('bnst', [128, 6], F32)
        bnag = sb('bnag', [128, 2], F32)
        qkT = [sb('qkT%d' % i, [128, 128], BF) for i in range(2)]
        rot = [sb('rot%d' % i, [128, 128], F32) for i in range(2)]
        rcs = [sb('rcs%d' % i, [128, 128], F32) for i in range(2)]
        gext = [sb('gext%d' % c, [128, CW - 1 + NMAX], F32) for c in range(4)]
        gtmp = sb('gtmp', [128, CW - 1], F32)

        cv = [a_e[0], a_e[1], a_x[0], a_x[1]]
        CVK = ['a_e0', 'a_e1', 'a_x0', 'a_x1']
        for c_ in range(4):
            memset(tmA[:, c_, :, 128:129], 1.0, ['tmA%d' % c_])
        P = [pst('ps%d' % i, [128, 512], F32) for i in range(7)]
        PB = pst('psb', [128, 1024], BF)
        rr = {'proj': 0, 'z': 0}

        def pbank(kind):
            if kind == 'proj':
                rr['proj'] ^= 1
                return rr['proj'], 'P%d' % rr['proj']
            rr['z'] ^= 1
            return 2 + rr['z'], 'P%d' % (2 + rr['z'])

        wrr = {'q2': 0, 'i': 0, 'b': 0, 'x': 0, 't': 0, 'k': 0, 'a': 0, 's': 0, 'q': 0, 'r': 0}

        def nxt(k, n=2):
            wrr[k] = (wrr[k] + 1) % n
            return wrr[k]

        def load_w(l, c0, ncols=512):
            if ncols == 8:
                S.dma('pool', wsm[:], I['w_in'][l, :, c0:c0 + 8].rearrange("(dc p) c -> p dc c", p=128), (), ['wsm'])
                return wsm, 'wsm'
            i = nxt('i')
            S.dma('pool', wb[i][:], I['w_in'][l, :, c0:c0 + ncols].rearrange("(dc p) c -> p dc c", p=128),
                  (), ['wb%d' % i])
            return wb[i], 'wb%d' % i

        def run_seq(l, q):
            N, L, nsb = q['N'], q['L'], q['nsb']
            nch = N // L
            last = (l == DEPTH - 1)
            go = l * 16

            for h in range(H):
                if q['init'] is None:
                    memset(Cext[h][:], 0.0, ['Cext%d' % h])
                    memset(Sst[h][:], 0.0, ['Sst%d' % h])
                else:
                    j = q['init']
                    S.dma('sp', Cext[h][:, 0:128], I['stC'][l, j, h], (), ['Cext%d' % h])
                    col = (l * NS + j) * H + h
                    cp(Cext[h][:, 128:129], stn[:, col:col + 1], ['par', 'Cext%d' % h], ['Cext%d' % h])
                    S.dma('sp', Sst[h][:], I['stS'][l, j, h], (), ['Sst%d' % h])
                cp(Cbf[h][:, 0:129], Cext[h][:], ['Cext%d' % h], ['Cbf%d' % h])
                cp(Sbf[h][:], Sst[h][:], ['Sst%d' % h], ['Sbf%d' % h])
            memset(Fc[:], 0.0, ['Fc'])
            if q['init'] is None:
                memset(Mcc[:], 0.0, ['Mcc'])
                memset(Mcr[:], 0.0, ['Mcr'])
            else:
                col = (l * NS + q['init']) * H
                cp(Mcc[:], stm[:, col:col + H], ['par'], ['Mcc'])
                cp(Mcr[:], stm[0:1, col:col + H], ['par'], ['Mcr'])
            for c in range(4):
                if q['init'] is None:
                    memset(gext[c][:, 0:CW - 1], 0.0, ['gext%d' % c])
                else:
                    S.dma('sp', gext[c][:, 0:CW - 1], I['stconv'][l, q['init'], c], (), ['gext%d' % c])

            xin = q['xin'][l]
            xout = q['xout'][l]

            for sbi in range(int(os.environ.get('MK_SB0', '0')) if q['init'] is None else 0, nsb):
                t0 = sbi * N
                pj, pk = pbank('proj')
                ssq = P[pj]
                for dc in range(NDC):
                    i = nxt('x')
                    S.dma('sp', xs[i][:, 0:N], xin[dc * 128:(dc + 1) * 128, t0:t0 + N], (), ['xs%d' % i])
                    act(sqb[i][:, 0:N], xs[i][:, 0:N], AF.Square, ['xs%d' % i], ['sqb%d' % i])
                    ts(xb[:, dc, 0:N], xs[i][:, 0:N], gcol[:, go + dc:go + dc + 1], ALU.mult,
                       ['xs%d' % i, 'par'], ['xb%d' % dc])
                    mm(ssq[:, 0:N], CB('onesb'), sqb[i][:, 0:N], dc == 0, dc == NDC - 1, ['cb', 'sqb%d' % i], [pk])
                    for c in range(nch):
                        mm(P[2 + c][0:L, 0:1], sqb[i][:, c * L:(c + 1) * L], CB('onesb', 128, 0, 1),
                           dc == 0, dc == NDC - 1, ['cb', 'sqb%d' % i], ['P%d' % (2 + c)])
                XB = ['xb%d' % dc for dc in range(NDC)]
                ts(rstd_bc[:, 0:N], ssq[:, 0:N], 1.0 / D, ALU.mult, [pk], ['rstd_bc'], s2=EPS, op1=ALU.add)
                act(rstd_bc[:, 0:N], rstd_bc[:, 0:N], AF.Sqrt, ['rstd_bc'], ['rstd_bc'])
                S.op('dve', lambda e: e.reciprocal(out=rstd_bc[:, 0:N], in_=rstd_bc[:, 0:N]), ['rstd_bc'], ['rstd_bc'])
                for c in range(nch):
                    ts(rstd_col[0:L, c:c + 1], P[2 + c][0:L, 0:1], 1.0 / D, ALU.mult, ['P%d' % (2 + c)], ['rstd_col'],
                       s2=EPS, op1=ALU.add)
                act(rstd_col[0:L, 0:nch], rstd_col[0:L, 0:nch], AF.Sqrt, ['rstd_col'], ['rstd_col'])
                S.op('dve', lambda e: e.reciprocal(out=rstd_col[0:L, 0:nch], in_=rstd_col[0:L, 0:nch]),
                     ['rstd_col'], ['rstd_col'])

                def fm_proj(w, wk, hcol):
                    pj, pk = pbank('proj')
                    for dc in range(NDC):
                        mm(P[pj][:, 0:N], w[:, dc, hcol:hcol + 128], xb[:, dc, 0:N], dc == 0, dc == NDC - 1,
                           [wk, 'xb%d' % dc], [pk])
                    return P[pj], pk

                def tm_proj(w, wk, c, ncols=512):
                    pj, pk = pbank('proj')
                    for dc in range(NDC):
                        mm(P[pj][0:L, 0:ncols], xb[:, dc, c * L:(c + 1) * L], w[:, dc, 0:ncols], dc == 0,
                           dc == NDC - 1, [wk, 'xb%d' % dc], [pk])
                    return P[pj], pk

                def gates_and_proj(br):
                    for g4 in range(4):
                        w, wk = load_w(l, OG + br * 2048 + g4 * 512)
                        bi = nxt('b')
                        S.dma('pool', wbr[bi][:], I['w_br'][l, br, :, g4 * 512:(g4 + 1) * 512]
                              .rearrange("(ec p) c -> p ec c", p=128), (), ['wbr%d' % bi])
                        for m4 in range(4):
                            mc = g4 * 4 + m4
                            ps, pk = fm_proj(w, wk, m4 * 128)
                            si = nxt('s')
                            tt(sig[si][:, 0:N], ps[:, 0:N], rstd_bc[:, 0:N], ALU.mult, [pk, 'rstd_bc'], ['sig%d' % si])
                            act(sig[si][:, 0:N], sig[si][:, 0:N], AF.Sigmoid, ['sig%d' % si], ['sig%d' % si])
                            pj, pk2 = pbank('proj')
                            for ec in range(4):
                                mm(P[pj][:, 0:N], wbr[bi][:, ec, m4 * 128:(m4 + 1) * 128], yT[:, ec, 0:N],
                                   ec == 0, ec == 3, ['wbr%d' % bi, 'yT%d' % ec], [pk2])
                            if br == 0:
                                tt(merged[:, mc, 0:N], P[pj][:, 0:N], sig[si][:, 0:N], ALU.mult,
                                   [pk2, 'sig%d' % si], ['mg%d' % mc])
                            else:
                                tt(sig[si][:, 0:N], P[pj][:, 0:N], sig[si][:, 0:N], ALU.mult,
                                   [pk2, 'sig%d' % si], ['sig%d' % si])
                                tt(merged[:, mc, 0:N], merged[:, mc, 0:N], sig[si][:, 0:N], ALU.add,
                                   ['mg%d' % mc, 'sig%d' % si], ['mg%d' % mc])

                def headnorm_to_yT(h, c, src, srck):
                    S.op('dve', lambda e: e.bn_stats(out=bnst[0:L, :], in_=src), [srck], ['bnst'])
                    S.op('dve', lambda e: e.bn_aggr(out=bnag[0:L, :], in_=bnst[0:L, :]), ['bnst'], ['bnag'])
                    ts(sm[0:L, 40:41], bnag[0:L, 1:2], LNEPS, ALU.add, ['bnag'], ['sm40'])
                    act(sm[0:L, 40:41], sm[0:L, 40:41], AF.Sqrt, ['sm40'], ['sm40'])
                    S.op('dve', lambda e: e.reciprocal(out=sm[0:L, 40:41], in_=sm[0:L, 40:41]), ['sm40'], ['sm40'])
                    ts(hn[0:L, :], src, bnag[0:L, 0:1], ALU.subtract, [srck, 'bnag', 'sm40'], ['hn'],
                       s2=sm[0:L, 40:41], op1=ALU.mult)
                    tr(PB[:, 0:L], hn[0:L, :], CB('identb', L, 0, L), ['hn', 'cb'], ['PB'])
                    tt(yT[:, h, c * L:(c + 1) * L], PB[:, 0:L], gate[:, h, c * L:(c + 1) * L], ALU.mult,
                       ['PB', 'gate%d' % h], ['yT%d' % h])

                w, wk = load_w(l, OA)
                for h in range(H):
                    ps, pk = fm_proj(w, wk, h * 128)
                    stt(fmA[:, h, 0:N], ps[:, 0:N], SC, rstd_bc[:, 0:N], ALU.mult, ALU.mult, [pk, 'rstd_bc'], ['fmA%d' % h])
                w, wk = load_w(l, OA + 512)
                for h in range(H):
                    ps, pk = fm_proj(w, wk, h * 128)
                    fi = nxt('t')
                    tt(f32t[fi][:, 0:N], ps[:, 0:N], rstd_bc[:, 0:N], ALU.mult, [pk, 'rstd_bc'], ['f32t%d' % fi])
                    S.dma('sp', q['k_out'][l][h][:, t0:t0 + N], f32t[fi][:, 0:N], ['f32t%d' % fi], ['kout'])
                    cp(fmB[:, h, 0:N], f32t[fi][:, 0:N], ['f32t%d' % fi], ['fmB%d' % h])
                    S.dma('sp', q['kscr'][l][h][:, q['kbase'] + t0:q['kbase'] + t0 + N], fmB[:, h, 0:N],
                          ['fmB%d' % h], ['kscr'])
                w, wk = load_w(l, OA + 1024)
                for c in range(nch):
                    ps, pk = tm_proj(w, wk, c)
                    fi = nxt('r')
                    ts(tmF[fi][0:L, :], ps[0:L, :], rstd_col[0:L, c:c + 1], ALU.mult, [pk, 'rstd_col'], ['tmF%d' % fi])
                    S.dma('sp', q['v_out'][l][t0 + c * L:t0 + (c + 1) * L, :], tmF[fi][0:L, :], ['tmF%d' % fi], ['vout'])
                    cp(tmB[0:L, c, :], tmF[fi][0:L, :], ['tmF%d' % fi], ['tmB%d' % c])
                    S.dma('sp', q['vscr'][l][q['kbase'] + t0 + c * L:q['kbase'] + t0 + (c + 1) * L, :], tmB[0:L, c, :],
                          ['tmB%d' % c], ['vscr'])
                w, wk = load_w(l, OA + 1536)
                for h in range(H):
                    ps, pk = fm_proj(w, wk, h * 128)
                    fi = nxt('t')
                    tt(f32t[fi][:, 0:N], ps[:, 0:N], rstd_bc[:, 0:N], ALU.mult, [pk, 'rstd_bc'], ['f32t%d' % fi])
                    act(gate[:, h, 0:N], f32t[fi][:, 0:N], AF.Silu, ['f32t%d' % fi], ['gate%d' % h])

                nkeys = q['kbase'] + t0 + N
                blocks = []
                kpos = nkeys
                while kpos > 0:
                    ks = min(128, kpos - ((kpos - 1) // 128) * 128)
                    blocks.append((kpos - ks, ks))
                    kpos -= ks
                if os.environ.get('MK_NBLK'):
                    blocks = blocks[:int(os.environ['MK_NBLK'])]
                if os.environ.get('MK_DIAGONLY'):
                    blocks = [b_ for b_ in blocks if b_[0] >= q['kbase'] + t0]
                for h in range(H):
                    first = True
                    ci = 0
                    for bi_, (k0, ks) in enumerate(blocks):
                        diag = (k0 >= q['kbase'] + t0)
                        jd = (k0 - q['kbase'] - t0) // 128 if diag else -1
                        kk = nxt('k')
                        S.dma('sp', kblk[kk][:, 0:ks], q['kscr'][l][h][:, k0:k0 + ks], ['kscr'], ['kblk%d' % kk])
                        S.dma('sp', vblk[kk][0:ks, 0, :], q['vscr'][l][k0:k0 + ks, h * 128:(h + 1) * 128], ['vscr'],
                              ['vblk%d' % kk])
                        zj, zk = pbank('z')
                        Z = P[zj]
                        mm(Z[0:ks, 0:N], kblk[kk][:, 0:ks], fmA[:, h, 0:N], True, True,
                           ['kblk%d' % kk, 'fmA%d' % h], [zk])
                        ai = nxt('a')
                        act(a_e[ai][0:ks, 0:N], Z[0:ks, 0:N], AF.Exp, [zk], ['a_e%d' % ai])
                        act(a_sp[ai][0:ks, 0:N], a_e[ai][0:ks, 0:N], AF.Ln, ['a_e%d' % ai], ['a_sp%d' % ai], bias=1.0)
                        if diag:
                            tt(a_sp[ai][0:ks, 0:N], a_sp[ai][0:ks, 0:N], CB('m01_%d' % jd, ks, 0, N), ALU.mult,
                               ['a_sp%d' % ai, 'cb'], ['a_sp%d' % ai])
                        else:
                            ts(a_sp[ai][0:ks, 0:N], a_sp[ai][0:ks, 0:N], 1.0, ALU.mult, ['a_sp%d' % ai], ['a_sp%d' % ai])
                        mm(P[4][0:ks, 0:N], CB('neguincl', ks, 0, ks), a_sp[ai][0:ks, 0:N], True, True,
                           ['cb', 'a_sp%d' % ai], ['P4'])
                        mm(P[5][:, 0:N], CB('negonesb', ks, 0, 128), a_sp[ai][0:ks, 0:N], True, True,
                           ['cb', 'a_sp%d' % ai], ['P5'])
                        if first:
                            assert diag
                            tt(a_x[ai][0:ks, 0:N], Z[0:ks, 0:N], CB('negm%d' % jd, ks, 0, N), ALU.add, [zk, 'cb'],
                               ['a_x%d' % ai])
                        else:
                            tt(a_x[ai][0:ks, 0:N], Z[0:ks, 0:N], carry[ci][0:ks, 0:N], ALU.add,
                               [zk, 'carry%d' % ci], ['a_x%d' % ai])
                            if diag:
                                tt(a_x[ai][0:ks, 0:N], a_x[ai][0:ks, 0:N], CB('negm%d' % jd, ks, 0, N), ALU.add,
                                   ['a_x%d' % ai, 'cb'], ['a_x%d' % ai])
                        tt(a_x[ai][0:ks, 0:N], a_x[ai][0:ks, 0:N], P[4][0:ks, 0:N], ALU.add, ['a_x%d' % ai, 'P4'],
                           ['a_x%d' % ai])
                        act(a_w[ai][0:ks, 0:N], a_x[ai][0:ks, 0:N], AF.Exp, ['a_x%d' % ai], ['a_w%d' % ai])
                        if bi_ < len(blocks) - 1:
                            if first:
                                cp(carry[1 - ci][:, 0:N], P[5][:, 0:N], ['P5'], ['carry%d' % (1 - ci)])
                            else:
                                tt(carry[1 - ci][:, 0:N], carry[ci][:, 0:N], P[5][:, 0:N], ALU.add,
                                   ['carry%d' % ci, 'P5'], ['carry%d' % (1 - ci)])
                            ci = 1 - ci
                        mm(P[6][:, 0:N], vblk[kk][0:ks, 0, :], a_w[ai][0:ks, 0:N], first, bi_ == len(blocks) - 1,
                           ['vblk%d' % kk, 'a_w%d' % ai], ['P6'])
                        first = False
                    tt(yT[:, h, 0:N], P[6][:, 0:N], gate[:, h, 0:N], ALU.mult, ['P6', 'gate%d' % h], ['yT%d' % h])
                gates_and_proj(0)

                w, wk = load_w(l, OB)
                for h in range(H):
                    ps, pk = fm_proj(w, wk, h * 128)
                    tt(fmA[:, h, 0:N], ps[:, 0:N], rstd_bc[:, 0:N], ALU.mult, [pk, 'rstd_bc'], ['fmA%d' % h])
                w, wk = load_w(l, OB + 512)
                for h in range(H):
                    ps, pk = fm_proj(w, wk, h * 128)
                    tt(fmB[:, h, 0:N], ps[:, 0:N], rstd_bc[:, 0:N], ALU.mult, [pk, 'rstd_bc'], ['fmB%d' % h])
                for c in range(nch):
                    ps, pk = tm_proj(w, wk, c)
                    ts(tmB[0:L, c, :], ps[0:L, :], rstd_col[0:L, c:c + 1], ALU.mult, [pk, 'rstd_col'], ['tmB%d' % c])
                w, wk = load_w(l, OB + 1024)
                for c in range(nch):
                    ps, pk = tm_proj(w, wk, c)
                    for h in range(H):
                        ts(tmA[0:L, c, h, 0:128], ps[0:L, h * 128:(h + 1) * 128], rstd_col[0:L, c:c + 1], ALU.mult,
                           [pk, 'rstd_col'], ['tmA%d' % c])
                w, wk = load_w(l, OB + 1536)
                for h in range(H):
                    ps, pk = fm_proj(w, wk, h * 128)
                    fi = nxt('t')
                    tt(f32t[fi][:, 0:N], ps[:, 0:N], rstd_bc[:, 0:N], ALU.mult, [pk, 'rstd_bc'], ['f32t%d' % fi])
                    act(gate[:, h, 0:N], f32t[fi][:, 0:N], AF.Sigmoid, ['f32t%d' % fi], ['gate%d' % h])
                w, wk = load_w(l, OB + 2048)
                for h in range(H):
                    ps, pk = fm_proj(w, wk, h * 128)
                    fi = nxt('t')
                    tt(f32t[fi][:, 0:N], ps[:, 0:N], rstd_bc[:, 0:N], ALU.mult, [pk, 'rstd_bc'], ['f32t%d' % fi])
                    act(f32t[fi][:, 0:N], f32t[fi][:, 0:N], AF.Silu, ['f32t%d' % fi], ['f32t%d' % fi])
                    tt(gate[:, h, 0:N], gate[:, h, 0:N], f32t[fi][:, 0:N], ALU.mult, ['gate%d' % h, 'f32t%d' % fi],
                       ['gate%d' % h])
                w, wk = load_w(l, OIF, 8)
                for c in range(nch):
                    ps, pk = tm_proj(w, wk, c, 8)
                    ts(ifc[0:L, c, :], ps[0:L, 0:8], rstd_col[0:L, c:c + 1], ALU.mult, [pk, 'rstd_col'], ['ifc%d' % c])
                for c in range(nch):
                    for h in range(H):
                        bo = l * 8
                        A_ = P[2]
                        act(sm[0:L, 0:1], ifc[0:L, c, 4 + h:5 + h], AF.Exp, ['ifc%d' % c, 'par2'], ['sm0'], scale=-1.0,
                            bias=nbf[0:L, bo + 4 + h:bo + 5 + h])
                        act(sm[0:L, 1:2], sm[0:L, 0:1], AF.Ln, ['sm0'], ['sm1'], bias=1.0)
                        mm(A_[0:L, 0:1], CF('negtri', L, 0, L), sm[0:L, 1:2], True, True, ['cf', 'sm1'], ['P2'])
                        mm(A_[:, 1:2], CF('negones', L, 0, 128), sm[0:L, 1:2], True, True, ['cf', 'sm1'], ['P2'])
                        tt(sm[0:L, 2:3], A_[0:L, 0:1], Fc[0:L, h:h + 1], ALU.add, ['P2', 'Fc'], ['sm2'])
                        tt(Fc[:, h:h + 1], Fc[:, h:h + 1], A_[:, 1:2], ALU.add, ['Fc', 'P2', 'sm2'], ['Fc'])
                        stt(sm[0:L, 3:4], ifc[0:L, c, h:h + 1], bif[0:L, bo + h:bo + h + 1], sm[0:L, 2:3], ALU.add,
                            ALU.subtract, ['ifc%d' % c, 'par', 'sm2'], ['sm3'])
                        mm(A_[0:1, 16:16 + L], sm[0:L, 3:4], CF('ident', L, 0, L), True, True, ['sm3', 'cf'], ['P2'])
                        S.op('dve', lambda e, h=h: e.tensor_tensor_scan(out=Mrow[0:1, 0:L], data0=CF('zeros', 1, 0, L),
                                                                         data1=A_[0:1, 16:16 + L],
                                                                         initial=Mcr[0:1, h:h + 1], op0=ALU.add,
                                                                         op1=ALU.max),
                             ['P2', 'cf', 'Mcr'], ['Mrow'])
                        mm(P[3][0:L, 0:L], CF('ones', 1, 0, L), Mrow[0:1, 0:L], True, True, ['cf', 'Mrow'], ['P3'])
                        mm(A_[0:L, 2:3], Mrow[0:1, 0:L], CF('ones', 1, 0, 1), True, True, ['cf', 'Mrow'], ['P2'])
                        mm(A_[:, 3:4], CF('ones', 1, 0, 128), Mrow[0:1, L - 1:L], True, True, ['cf', 'Mrow'], ['P2'])
                        cp(Mcr[0:1, h:h + 1], Mrow[0:1, L - 1:L], ['Mrow'], ['Mcr'])
                        act(Dt[0:L, 0:L], P[3][0:L, 0:L], AF.Exp, ['P3', 'sm3'], ['Dt'], scale=-1.0, bias=sm[0:L, 3:4])
                        mm(P[3][0:L, 128:128 + L], fmB[:, h, c * L:(c + 1) * L], fmA[:, h, c * L:(c + 1) * L], True, True,
                           ['fmB%d' % h, 'fmA%d' % h], ['P3'])
                        tt(Dt[0:L, 0:L], Dt[0:L, 0:L], CF('masksc', L, 0, L), ALU.mult, ['Dt', 'cf'], ['Dt'])
                        tt(Wt[0:L, 0:L], Dt[0:L, 0:L], P[3][0:L, 128:128 + L], ALU.mult, ['Dt', 'P3'], ['Wt'])
                        mm(P[4][0:L, 0:129], Wt[0:L, 0:L], tmA[0:L, c, h, 0:129], True, True, ['Wt', 'tmA%d' % c], ['P4'])
                        mm(P[5][0:L, 0:129], fmA[:, h, c * L:(c + 1) * L], Cbf[h][:, 0:129], True, True,
                           ['fmA%d' % h, 'Cbf%d' % h], ['P5'])
                        act(sm[0:L, 4:5], A_[0:L, 2:3], AF.Exp, ['P2', 'Mcc'], ['sm4'], scale=-1.0,
                            bias=Mcc[0:L, h:h + 1])
                        act(nd[0:L, 0:129], P[5][0:L, 0:129], AF.Copy, ['P5', 'sm4'], ['nd'], scale=sm[0:L, 4:5])
                        tt(nd[0:L, 0:129], nd[0:L, 0:129], P[4][0:L, 0:129], ALU.add, ['nd', 'P4'], ['nd'])
                        tt(sm[0:L, 5:6], sm[0:L, 2:3], A_[0:L, 2:3], ALU.add, ['sm2', 'P2'], ['sm5'])
                        act(sm[0:L, 6:7], sm[0:L, 5:6], AF.Exp, ['sm5'], ['sm6'], scale=-1.0)
                        ts(sm[0:L, 7:8], nd[0:L, 128:129], -1.0, ALU.mult, ['nd'], ['sm7'])
                        tt(sm[0:L, 7:8], sm[0:L, 7:8], nd[0:L, 128:129], ALU.max, ['nd', 'sm7'], ['sm7'])
                        tt(sm[0:L, 7:8], sm[0:L, 7:8], sm[0:L, 6:7], ALU.max, ['sm6', 'sm7'], ['sm7'])
                        S.op('dve', lambda e: e.reciprocal(out=sm[0:L, 7:8], in_=sm[0:L, 7:8]), ['sm7'], ['sm7'])
                        ts(hh[0:L, :], nd[0:L, 0:128], sm[0:L, 7:8], ALU.mult, ['nd', 'sm7'], ['hh'])
                        headnorm_to_yT(h, c, hh[0:L, :], 'hh')
                        ts(sm[:, 8:9], A_[:, 3:4], -1.0, ALU.mult, ['P2'], ['sm8'])
                        act(sm[0:L, 9:10], sm[0:L, 3:4], AF.Exp, ['sm3', 'sm8'], ['sm9'], bias=sm[0:L, 8:9])
                        ts(kw[0:L, :], tmB[0:L, c, h * 128:(h + 1) * 128], sm[0:L, 9:10], ALU.mult, ['tmB%d' % c, 'sm9'],
                           ['kw'], s2=SC, op1=ALU.mult)
                        mm(P[4][:, 256:385], kw[0:L, :], tmA[0:L, c, h, 0:129], True, True, ['kw', 'tmA%d' % c], ['P4'])
                        act(sm[:, 10:11], Mcc[:, h:h + 1], AF.Exp, ['Mcc', 'sm8'], ['sm10'], bias=sm[:, 8:9])
                        stt(Cext[h][:], Cext[h][:], sm[:, 10:11], P[4][:, 256:385], ALU.mult, ALU.add,
                            ['Cext%d' % h, 'sm10', 'P4'], ['Cext%d' % h])
                        cp(Cbf[h][:, 0:129], Cext[h][:], ['Cext%d' % h], ['Cbf%d' % h])
                        ts(Mcc[:, h:h + 1], sm[:, 8:9], -1.0, ALU.mult, ['sm8', 'sm4', 'sm10'], ['Mcc'])
                gates_and_proj(1)

                lg = _gammas()
                for which, off, dst in (('q', OC, tmC), ('k', OC + 512, tmB)):
                    w, wk = load_w(l, off)
                    for c in range(nch):
                        ps, pk = tm_proj(w, wk, c)
                        ri = nxt('q')
                        r0 = q['pos0'] + t0 + c * L
                        S.dma('sp', rcs[ri][0:L, 0:64], I['rcos'][r0:r0 + L, :], (), ['rcs%d' % ri])
                        S.dma('sp', rcs[ri][0:L, 64:128], I['rsin'][r0:r0 + L, :], (), ['rcs%d' % ri])
                        fi = nxt('r')
                        ts(tmF[fi][0:L, :], ps[0:L, :], rstd_col[0:L, c:c + 1], ALU.mult, [pk, 'rstd_col'], ['tmF%d' % fi])
                        for h in range(H):
                            x1 = tmF[fi][0:L, h * 128:h * 128 + 64]
                            x2 = tmF[fi][0:L, h * 128 + 64:h * 128 + 128]
                            cs, sn = rcs[ri][0:L, 0:64], rcs[ri][0:L, 64:128]
                            ri2 = nxt('q2')
                            R = rot[ri2]
                            rk = 'rot%d' % ri2
                            tt(R[0:L, 0:64], x1, cs, ALU.mult, ['tmF%d' % fi, 'rcs%d' % ri], [rk])
                            tt(R[0:L, 64:128], x2, sn, ALU.mult, ['tmF%d' % fi, 'rcs%d' % ri], [rk])
                            tt(R[0:L, 0:64], R[0:L, 0:64], R[0:L, 64:128], ALU.subtract, [rk], [rk])
                            tt(R[0:L, 64:128], x1, sn, ALU.mult, ['tmF%d' % fi, 'rcs%d' % ri, rk], [rk])
                            tt(hh[0:L, 0:64], x2, cs, ALU.mult, ['tmF%d' % fi, 'rcs%d' % ri], ['hh'])
                            tt(R[0:L, 64:128], R[0:L, 64:128], hh[0:L, 0:64], ALU.add, [rk, 'hh'], [rk])
                            dec = CF('qdec' if which == 'q' else 'kdec', L, h, h + 1)
                            ts(dst[0:L, c, h * 128:(h + 1) * 128], R[0:L, :], dec, ALU.mult, [rk, 'cf'],
                               ['%s%d' % ('tmC' if which == 'q' else 'tmB', c)])
                w, wk = load_w(l, OC + 1024)
                for c in range(nch):
                    ps, pk = tm_proj(w, wk, c)
                    for h in range(H):
                        ts(tmA[0:L, c, h, 0:128], ps[0:L, h * 128:(h + 1) * 128], rstd_col[0:L, c:c + 1], ALU.mult,
                           [pk, 'rstd_col'], ['tmA%d' % c])
                w, wk = load_w(l, OC + 1536)
                for h in range(H):
                    ps, pk = fm_proj(w, wk, h * 128)
                    fi = nxt('t')
                    tt(f32t[fi][:, 0:N], ps[:, 0:N], rstd_bc[:, 0:N], ALU.mult, [pk, 'rstd_bc'], ['f32t%d' % fi])
                    act(gate[:, h, 0:N], f32t[fi][:, 0:N], AF.Silu, ['f32t%d' % fi], ['gate%d' % h])
                for c in range(nch):
                    for h in range(H):
                        gL = float(np.exp(lg[h] * L))
                        tr(PB[:, 0:L], tmC[0:L, c, h * 128:(h + 1) * 128], CB('identb', L, 0, L), ['tmC%d' % c, 'cb'], ['PB'])
                        cp(qkT[0][:, 0:L], PB[:, 0:L], ['PB'], ['qkT0'])
                        tr(PB[:, 512:512 + L], tmB[0:L, c, h * 128:(h + 1) * 128], CB('identb', L, 0, L),
                           ['tmB%d' % c, 'cb'], ['PB'])
                        act(qkT[1][:, 0:L], PB[:, 512:512 + L], AF.Copy, ['PB'], ['qkT1'])
                        mm(P[3][0:L, 0:L], qkT[1][:, 0:L], qkT[0][:, 0:L], True, True, ['qkT0', 'qkT1'], ['P3'])
                        tt(Wt[0:L, 0:L], P[3][0:L, 0:L], CF('mask01', L, 0, L), ALU.mult, ['P3', 'cf'], ['Wt'])
                        mm(P[4][0:L, 0:128], Wt[0:L, 0:L], tmA[0:L, c, h, 0:128], True, False, ['Wt', 'tmA%d' % c], ['P4'])
                        mm(P[4][0:L, 0:128], qkT[0][:, 0:L], Sbf[h][:], False, True, ['qkT0', 'Sbf%d' % h], ['P4'])
                        cp(hh[0:L, :], P[4][0:L, 0:128], ['P4'], ['hh'])
                        headnorm_to_yT(h, c, hh[0:L, :], 'hh')
                        mm(P[4][:, 256:384], tmB[0:L, c, h * 128:(h + 1) * 128], tmA[0:L, c, h, 0:128], True, True,
                           ['tmB%d' % c, 'tmA%d' % c], ['P4'])
                        tt(Sst[h][:], Sst[h][:], P[4][:, 256:384], ALU.add, ['Sst%d' % h, 'P4'], ['Sst%d' % h])
                        ts(Sst[h][:], Sst[h][:], gL, ALU.mult, ['Sst%d' % h], ['Sst%d' % h])
                        cp(Sbf[h][:], Sst[h][:], ['Sst%d' % h], ['Sbf%d' % h])
                gates_and_proj(2)

                w, wk = load_w(l, OD)
                w2, wk2 = load_w(l, OD + 512)
                for c4 in range(4):
                    ps, pk = fm_proj(w, wk, c4 * 128)
                    tt(gext[c4][:, CW - 1:CW - 1 + N], ps[:, 0:N], rstd_bc[:, 0:N], ALU.mult, [pk, 'rstd_bc'],
                       ['gext%d' % c4])
                    ps, pk = fm_proj(w2, wk2, c4 * 128)
                    fi = nxt('t')
                    tt(f32t[fi][:, 0:N], ps[:, 0:N], rstd_bc[:, 0:N], ALU.mult, [pk, 'rstd_bc'], ['f32t%d' % fi])
                    act(f32t[fi][:, 0:N], f32t[fi][:, 0:N], AF.Sigmoid, ['f32t%d' % fi], ['f32t%d' % fi])
                    tt(gext[c4][:, CW - 1:CW - 1 + N], gext[c4][:, CW - 1:CW - 1 + N], f32t[fi][:, 0:N], ALU.mult,
                       ['gext%d' % c4, 'f32t%d' % fi], ['gext%d' % c4])
                w, wk = load_w(l, OD + 1024)
                for c4 in range(4):
                    ps, pk = fm_proj(w, wk, c4 * 128)
                    fi = nxt('t')
                    tt(f32t[fi][:, 0:N], ps[:, 0:N], rstd_bc[:, 0:N], ALU.mult, [pk, 'rstd_bc'], ['f32t%d' % fi])
                    act(fmB[:, c4, 0:N], f32t[fi][:, 0:N], AF.Silu, ['f32t%d' % fi], ['fmB%d' % c4])
                for c4 in range(4):
                    wo = (l * 4 + c4) * CW
                    ts(cv[c4][:, 0:N], gext[c4][:, 0:N], convw[:, wo:wo + 1], ALU.mult, ['gext%d' % c4, 'par'],
                       [CVK[c4]], s2=convb[:, l * 4 + c4:l * 4 + c4 + 1], op1=ALU.add)
                    for k in range(1, CW):
                        stt(cv[c4][:, 0:N], gext[c4][:, k:k + N], convw[:, wo + k:wo + k + 1], cv[c4][:, 0:N],
                            ALU.mult, ALU.add, ['gext%d' % c4, 'par', CVK[c4]], [CVK[c4]])
                    cp(gtmp[:], gext[c4][:, N:N + CW - 1], ['gext%d' % c4], ['gtmp'])
                    cp(gext[c4][:, 0:CW - 1], gtmp[:], ['gtmp'], ['gext%d' % c4])
                for c4 in range(4):
                    mm(P[2][:, 0:N], CF('o512'), cv[c4][:, 0:N], c4 == 0, c4 == 3, ['cf', CVK[c4]], ['P2'])
                for c4 in range(4):
                    fi = nxt('t')
                    act(f32t[fi][:, 0:N], cv[c4][:, 0:N], AF.Square, [CVK[c4]], ['f32t%d' % fi])
                    mm(P[3][:, 0:N], CF('o512'), f32t[fi][:, 0:N], c4 == 0, c4 == 3, ['cf', 'f32t%d' % fi], ['P3'])
                m_ = sig[0]
                r_ = sig[1]
                cp(m_[:, 0:N], P[2][:, 0:N], ['P2'], ['sig0'])
                tt(r_[:, 0:N], m_[:, 0:N], m_[:, 0:N], ALU.mult, ['sig0'], ['sig1'])
                tt(r_[:, 0:N], P[3][:, 0:N], r_[:, 0:N], ALU.subtract, ['P3', 'sig1'], ['sig1'])
                ts(r_[:, 0:N], r_[:, 0:N], LNEPS, ALU.add, ['sig1'], ['sig1'])
                act(r_[:, 0:N], r_[:, 0:N], AF.Sqrt, ['sig1'], ['sig1'])
                S.op('dve', lambda e: e.reciprocal(out=r_[:, 0:N], in_=r_[:, 0:N]), ['sig1'], ['sig1'])
                for c4 in range(4):
                    fi = nxt('t')
                    tt(f32t[fi][:, 0:N], cv[c4][:, 0:N], m_[:, 0:N], ALU.subtract, [CVK[c4], 'sig0'], ['f32t%d' % fi])
                    tt(f32t[fi][:, 0:N], f32t[fi][:, 0:N], r_[:, 0:N], ALU.mult, ['f32t%d' % fi, 'sig1'], ['f32t%d' % fi])
                    ts(f32t[fi][:, 0:N], f32t[fi][:, 0:N], lng[:, l * 4 + c4:l * 4 + c4 + 1], ALU.mult,
                       ['f32t%d' % fi, 'par'], ['f32t%d' % fi], s2=lnb[:, l * 4 + c4:l * 4 + c4 + 1], op1=ALU.add)
                    act(f32t[fi][:, 0:N], f32t[fi][:, 0:N], AF.Silu, ['f32t%d' % fi], ['f32t%d' % fi])
                    tt(yT[:, c4, 0:N], f32t[fi][:, 0:N], fmB[:, c4, 0:N], ALU.mult, ['f32t%d' % fi, 'fmB%d' % c4],
                       ['yT%d' % c4])
                gates_and_proj(3)

                for mc in range(NDC):
                    cp(mbf[:, mc, 0:N], merged[:, mc, 0:N], ['mg%d' % mc], ['xb%d' % mc])
                for g4 in range(4):
                    i = nxt('i')
                    S.dma('pool', wb[i][:], I['w_out'][l, :, g4 * 512:(g4 + 1) * 512].rearrange("(dc p) c -> p dc c", p=128),
                          (), ['wb%d' % i])
                    for m4 in range(4):
                        mc = g4 * 4 + m4
                        pj, pk = pbank('proj')
                        for m2 in range(NDC):
                            mm(P[pj][:, 0:N], wb[i][:, m2, m4 * 128:(m4 + 1) * 128], mbf[:, m2, 0:N], m2 == 0,
                               m2 == NDC - 1, ['wb%d' % i, 'xb%d' % m2], [pk])
                        xi = nxt('x')
                        S.dma('sp', xs[xi][:, 0:N], xin[mc * 128:(mc + 1) * 128, t0:t0 + N], (), ['xs%d' % xi])
                        tt(merged[:, mc, 0:N], P[pj][:, 0:N], xs[xi][:, 0:N], ALU.add,
                           [pk, 'xs%d' % xi], ['mg%d' % mc])
                        if not last:
                            S.dma('sp', xout[mc * 128:(mc + 1) * 128, t0:t0 + N], merged[:, mc, 0:N], ['mg%d' % mc], ['xout'])
                if last:
                    pj, pk = pbank('proj')
                    for mc in range(NDC):
                        si = nxt('x')
                        act(sqb[si][:, 0:N], merged[:, mc, 0:N], AF.Square, ['mg%d' % mc], ['sqb%d' % si])
                        mm(P[pj][:, 0:N], CB('onesb'), sqb[si][:, 0:N], mc == 0, mc == NDC - 1, ['cb', 'sqb%d' % si], [pk])
                    ts(rstd_bc[:, 0:N], P[pj][:, 0:N], 1.0 / D, ALU.mult, [pk], ['rstd_bc'], s2=EPS, op1=ALU.add)
                    act(rstd_bc[:, 0:N], rstd_bc[:, 0:N], AF.Sqrt, ['rstd_bc'], ['rstd_bc'])
                    S.op('dve', lambda e: e.reciprocal(out=rstd_bc[:, 0:N], in_=rstd_bc[:, 0:N]), ['rstd_bc'], ['rstd_bc'])
                    for mc in range(NDC):
                        stt(merged[:, mc, 0:N], merged[:, mc, 0:N], fgcol[:, mc:mc + 1], rstd_bc[:, 0:N], ALU.mult,
                            ALU.mult, ['mg%d' % mc, 'par', 'rstd_bc'], ['mg%d' % mc])
                        S.dma('sp', xout[mc * 128:(mc + 1) * 128, t0:t0 + N], merged[:, mc, 0:N], ['mg%d' % mc], ['xout'])

            si_ = q['sidx']
            for h in range(H):
                S.dma('sp', O['Cn'][l, si_, h], Cext[h][:], ['Cext%d' % h], ['o_Cn'])
                S.dma('sp', O['S'][l, si_, h], Sst[h][:], ['Sst%d' % h], ['o_S'])
                tt(sm[:, 20 + h:21 + h], Fc[:, h:h + 1], Mcc[:, h:h + 1], ALU.add, ['Fc', 'Mcc'], ['sm2%d' % h])
                S.dma('sp', O['m'][l, si_, h], sm[:, 20 + h:21 + h], ['sm2%d' % h], ['o_m'])
            for c4 in range(4):
                S.dma('sp', O['conv'][l, si_, c4], gext[c4][:, 0:CW - 1], ['gext%d' % c4], ['o_conv'])

        def prep_sample(l, j):
            for kb in range(PAST // 128):
                kk = nxt('k')
                S.dma('pool', vblk[kk][:].rearrange("p a b -> p (a b)"), I['cache_k'][l, j, kb * 128:(kb + 1) * 128, :],
                      (), ['vblk%d' % kk])
                for h in range(H):
                    tr(PB[:, h * 128:(h + 1) * 128], vblk[kk][:, h, :], CB('identb'), ['vblk%d' % kk, 'cb'], ['PB'])
                ai = nxt('a')
                cp(a_sp[ai][:, 0:512], PB[:, 0:512], ['PB'], ['a_sp%d' % ai])
                for h in range(H):
                    S.dma('sp', KSS[l, j, h][:, kb * 128:(kb + 1) * 128], a_sp[ai][:, h * 128:(h + 1) * 128],
                          ['a_sp%d' % ai], ['kscr'])
                kk = nxt('k')
                S.dma('pool', vblk[kk][:].rearrange("p a b -> p (a b)"), I['cache_v'][l, j, kb * 128:(kb + 1) * 128, :],
                      (), ['vblk%d' % kk])
                S.dma('sp', VSS[l, j, kb * 128:(kb + 1) * 128, :], vblk[kk][:].rearrange("p a b -> p (a b)"),
                      ['vblk%d' % kk], ['vscr'])

        prompt = dict(N=512, L=128, nsb=NSB_RUN, init=None, sidx=0, pos0=0, kbase=0,
                      xin=[I['xT'], X1], xout=[X1, O['yT']],
                      k_out=[[O['kT'][l, h] for h in range(H)] for l in range(DEPTH)],
                      v_out=[O['v'][l] for l in range(DEPTH)],
                      kscr=[[KSC[l, h] for h in range(H)] for l in range(DEPTH)],
                      vscr=[VSC[l] for l in range(DEPTH)])
        for l in range(NLAYER):
            run_seq(l, prompt)
        if DO_SAMPLE:
            for j in range(NS):
                sq_ = dict(N=TS, L=TS, nsb=1, init=j, sidx=1 + j, pos0=T, kbase=PAST,
                           xin=[I['xsT'][j], X1s[j]], xout=[X1s[j], O['ysT'][j]],
                           k_out=[[O['ksT'][l, j, h] for h in range(H)] for l in range(DEPTH)],
                           v_out=[O['vs'][l, j] for l in range(DEPTH)],
                           kscr=[[KSS[l, j, h] for h in range(H)] for l in range(DEPTH)],
                           vscr=[VSS[l, j] for l in range(DEPTH)])
                for l in range(NLAYER):
                    prep_sample(l, j)
                    run_seq(l, sq_)
        S.finish()
        S.emit()
        print("program ops:", S.nops, {e: len(S.prog[e]) for e in S.ENG})
    return nc


def _col(a):
    return np.ascontiguousarray(a.reshape(-1, 128).T)


def kernel(x_prompt, x_sample, cache_sb_k, cache_sb_v, state_mlstm_C, state_mlstm_n, state_mlstm_m,
           state_ret_S, state_conv, norm_g, w_in, mlstm_b_i, mlstm_b_f, conv_w, conv_b,
           conv_ln_g, conv_ln_b, w_branch, w_out, final_g):
    f = lambda a: np.ascontiguousarray(np.asarray(a, np.float32))
    x_prompt, x_sample = f(x_prompt), f(x_sample)
    cfa, cba = _build_consts()
    rc, rs = _rot_tables()
    nc = build_program()
    gcol = np.concatenate([_col(f(norm_g)[l]) for l in range(DEPTH)], 1)
    fgcol = _col(f(final_g))
    bif = np.tile(np.concatenate([np.concatenate([f(mlstm_b_i)[l], f(mlstm_b_f)[l]]) for l in range(DEPTH)])[None, :], (128, 1))
    cw = f(conv_w)
    convw = np.ascontiguousarray(cw.reshape(DEPTH, CW, 4, 128).transpose(3, 0, 2, 1).reshape(128, DEPTH * 4 * CW))
    c4 = lambda a: np.ascontiguousarray(f(a).reshape(DEPTH, 4, 128).transpose(2, 0, 1).reshape(128, DEPTH * 4))
    shared = dict(w_in=f(w_in), w_br=f(w_branch), w_out=f(w_out), gcol=gcol, fgcol=fgcol, bif=np.ascontiguousarray(bif),
                  convw=convw, convb=c4(conv_b), lng=c4(conv_ln_g), lnb=c4(conv_ln_b), rcos=rc, rsin=rs, cf=cfa, cb=cba)
    xTs = [np.ascontiguousarray(x_prompt[b].T) for b in range(2)]
    in_maps = []
    for c in range(8):
        sl = slice(NS * c, NS * (c + 1))
        m = dict(shared)
        m['xT'] = xTs[c % 2]
        m['xsT'] = np.ascontiguousarray(x_sample[sl].transpose(0, 2, 1))
        m['cache_k'] = np.ascontiguousarray(f(cache_sb_k)[:, sl].reshape(DEPTH, NS, PAST, 512))
        m['cache_v'] = np.ascontiguousarray(f(cache_sb_v)[:, sl].reshape(DEPTH, NS, PAST, 512))
        m['stC'] = np.ascontiguousarray(f(state_mlstm_C)[:, sl])
        m['stn'] = np.ascontiguousarray(f(state_mlstm_n)[:, sl].reshape(DEPTH * NS * H, 128).T)
        m['stm'] = np.ascontiguousarray(np.tile(f(state_mlstm_m)[:, sl].reshape(1, DEPTH * NS * H), (128, 1)))
        m['stS'] = np.ascontiguousarray(f(state_ret_S)[:, sl])
        m['stconv'] = np.ascontiguousarray(f(state_conv)[:, sl].reshape(DEPTH, NS, CW - 1, 4, 128).transpose(0, 1, 3, 4, 2))
        in_maps.append(m)
    ncores = int(os.environ.get('MK_CORES', '8'))
    res = run_bass_kernel_spmd(nc, in_maps[:ncores], core_ids=list(range(ncores)))
    R = res.results
    if os.environ.get('MK_RAW'):
        return R
    B = 2
    y_prompt = np.stack([R[b]['yT'].T for b in range(B)])
    y_sample = np.concatenate([R[c]['ysT'].transpose(0, 2, 1) for c in range(8)], 0)
    pk = np.stack([np.stack([R[b]['kT_o'][l].transpose(2, 0, 1) for b in range(B)]) for l in range(DEPTH)])
    pv = np.stack([np.stack([R[b]['v_o'][l].reshape(T, H, HD) for b in range(B)]) for l in range(DEPTH)])
    pC = np.stack([np.stack([R[b]['Cn_o'][l, 0, :, :, 0:128] for b in range(B)]) for l in range(DEPTH)])
    pn = np.stack([np.stack([R[b]['Cn_o'][l, 0, :, :, 128] for b in range(B)]) for l in range(DEPTH)])
    pm = np.stack([np.stack([R[b]['m_o'][l, 0, :, 0, 0] for b in range(B)]) for l in range(DEPTH)])
    pS = np.stack([np.stack([R[b]['S_o'][l, 0] for b in range(B)]) for l in range(DEPTH)])
    cvt = lambda a: a.transpose(2, 0, 1).reshape(CW - 1, 512)
    pconv = np.stack([np.stack([cvt(R[b]['conv_o'][l, 0]) for b in range(B)]) for l in range(DEPTH)])
    sk = np.stack([np.concatenate([R[c]['ksT_o'][l].transpose(0, 3, 1, 2) for c in range(8)], 0) for l in range(DEPTH)])
    sv = np.stack([np.concatenate([R[c]['vs_o'][l].reshape(NS, TS, H, HD) for c in range(8)], 0) for l in range(DEPTH)])
    sC = np.stack([np.concatenate([R[c]['Cn_o'][l, 1:, :, :, 0:128] for c in range(8)], 0) for l in range(DEPTH)])
    sn = np.stack([np.concatenate([R[c]['Cn_o'][l, 1:, :, :, 128] for c in range(8)], 0) for l in range(DEPTH)])
    sm_ = np.stack([np.concatenate([R[c]['m_o'][l, 1:, :, 0, 0] for c in range(8)], 0) for l in range(DEPTH)])
    sS = np.stack([np.concatenate([R[c]['S_o'][l, 1:] for c in range(8)], 0) for l in range(DEPTH)])
    sconv = np.stack([np.concatenate([np.stack([cvt(R[c]['conv_o'][l, 1 + j]) for j in range(NS)]) for c in range(8)], 0)
                      for l in range(DEPTH)])
    outs = (y_prompt, y_sample, pk, pv, pC, pn, pm, pS, pconv, sk, sv, sC, sn, sm_, sS, sconv)
    return tuple(np.ascontiguousarray(o, dtype=np.float32) for o in outs)
```

```python
import os
import numpy as np
import ml_dtypes
from contextlib import ExitStack
import concourse.bass as bass
import concourse.mybir as mybir
from concourse.bass_utils import run_bass_kernel_spmd

F32 = mybir.dt.float32
BF = mybir.dt.bfloat16
AF = mybir.ActivationFunctionType
ALU = mybir.AluOpType

D = 2048
T = 16384
DEPTH = 2
H = 4
HD = 128
NIN = 16392
NS = 4
TS = 16
PAST = 4096
CW = 31
NDC = 16
EPS = 1e-6
LNEPS = 1e-5
SC = HD ** -0.5
OA = 0
OB = 2048
OIF = 4608
OC = 4616
OD = 6664
OG = 8200

NSB_RUN = int(os.environ.get("MK_NSB", "32"))
DO_SAMPLE = int(os.environ.get("MK_SAMPLE", "1"))
NLAYER = int(os.environ.get("MK_LAYERS", "2"))
NOSELF = int(os.environ.get("MK_NOSELF", "0"))


class Sched:
    ENG = ['pe', 'act', 'dve', 'pool', 'sp']
    NDMA = 24
    NQ = 8

    def __init__(self, nc, stack):
        self.nc = nc
        self.prog = {e: [] for e in self.ENG}
        self.cnt = {e: 0 for e in self.ENG}
        self.seen = {e: {} for e in self.ENG}
        self.last_w = {}
        self.readers = {}
        self.sem = {}
        for e in ['pe', 'act', 'dve', 'pool']:
            self.sem[e] = stack.enter_context(nc.semaphore('s_' + e))
        self.dcount = {}
        for i in range(self.NDMA):
            k = 'd%d' % i
            self.sem[k] = stack.enter_context(nc.semaphore('s_' + k))
            self.dcount[k] = 0
        self.dnext = 0
        self.qnext = 0
        for i in range(self.NQ):
            k = 'q%d' % i
            self.sem[k] = stack.enter_context(nc.semaphore('s_' + k))
            self.dcount[k] = 0
        self.nops = 0

    def _deps(self, eng, reads, writes):
        deps = set()
        for b in reads:
            if b in self.last_w:
                deps.add(self.last_w[b])
        for b in writes:
            if b in self.last_w:
                deps.add(self.last_w[b])
            for kv in self.readers.get(b, {}).items():
                deps.add(kv)
        for (k, v) in sorted(deps):
            if k == eng and (eng == 'pe' or NOSELF):
                continue
            if self.seen[eng].get(k, 0) < v:
                self.prog[eng].append(('wait', k, v))
                self.seen[eng][k] = v

    def _commit(self, me, reads, writes):
        for b in writes:
            self.last_w[b] = me
            self.readers[b] = {}
        for b in reads:
            self.readers.setdefault(b, {})[me[0]] = me[1]

    def op(self, eng, fn, reads=(), writes=()):
        self._deps(eng, reads, writes)
        self.cnt[eng] += 1
        self.prog[eng].append(('op', fn))
        self._commit((eng, self.cnt[eng]), reads, writes)
        self.nops += 1

    def dma(self, q, out, in_, reads=(), writes=()):
        if q == 'pool':
            d = 'q%d' % self.qnext
            self.qnext = (self.qnext + 1) % self.NQ
        else:
            d = 'd%d' % self.dnext
            self.dnext = (self.dnext + 1) % self.NDMA
        if self.dcount[d] > 0 and self.seen[q].get(d, 0) < 16 * self.dcount[d]:
            self.prog[q].append(('wait', d, 16 * self.dcount[d]))
            self.seen[q][d] = 16 * self.dcount[d]
        self._deps(q, reads, writes)
        self.dcount[d] += 1
        self.prog[q].append(('dma', out, in_, d))
        self._commit((d, 16 * self.dcount[d]), reads, writes)
        self.nops += 1

    def finish(self):
        for d, c in self.dcount.items():
            if c > 0 and self.seen['sp'].get(d, 0) < 16 * c:
                self.prog['sp'].append(('wait', d, 16 * c))
        for e in ['pe', 'act', 'dve', 'pool']:
            if self.cnt[e] > 0:
                self.prog['sp'].append(('wait', e, self.cnt[e]))

    def emit(self):
        nc = self.nc
        sem = self.sem

        def replay(name, e):
            for it in self.prog[name]:
                if it[0] == 'wait':
                    e.wait_ge(sem[it[1]], it[2])
                elif it[0] == 'op':
                    it[1](e).then_inc(sem[name], 1)
                elif it[0] == 'dma':
                    e.dma_start(out=it[1], in_=it[2]).then_inc(sem[it[3]], 16)

        with nc.Block() as block:
            @block.tensor
            def _(e):
                replay('pe', e)

            @block.scalar
            def _(e):
                replay('act', e)

            @block.vector
            def _(e):
                replay('dve', e)

            @block.gpsimd
            def _(e):
                replay('pool', e)

            @block.sync
            def _(e):
                replay('sp', e)


def _gammas():
    lg = np.log1p(-np.exp2(-5.0 - np.arange(H, dtype=np.float64)))
    return lg


CF_LAYOUT = {}
CB_LAYOUT = {}


def _build_consts():
    cf = []
    cb = []

    def addf(name, a):
        a = np.asarray(a, np.float32)
        CF_LAYOUT[name] = (sum(x.shape[1] for x in cf), a.shape[1])
        cf.append(a)

    def addb(name, a):
        a = np.asarray(a, np.float32)
        CB_LAYOUT[name] = (sum(x.shape[1] for x in cb), a.shape[1])
        cb.append(a)

    p = np.arange(128)[:, None]
    f = np.arange(128)[None, :]
    addf('ident', (p == f))
    addf('ones', np.ones((128, 128)))
    addf('zeros', np.zeros((128, 128)))
    addf('negtri', -(p <= f).astype(np.float32))
    addf('negones', -np.ones((128, 128)))
    addf('masksc', (p <= f) * SC)
    addf('mask01', (p <= f))
    addf('o512', np.full((128, 128), 1.0 / 512.0))
    lg = _gammas()
    t1 = np.arange(128, dtype=np.float64)[:, None] + 1.0
    addf('qdec', np.exp(lg[None, :] * t1))
    addf('kdec', np.exp(-lg[None, :] * t1) * SC)
    q = np.arange(512)[None, :]
    for j in range(4):
        valid = (p + 128 * j) < q
        addb('negm%d' % j, np.where(valid, 0.0, -30000.0))
        addb('m01_%d' % j, valid)
    addb('identb', (p == f))
    addb('onesb', np.ones((128, 128)))
    addb('neguincl', -(p >= f).astype(np.float32))
    addb('negonesb', -np.ones((128, 128)))
    cfa = np.concatenate(cf, 1).astype(np.float32)
    cba = np.concatenate(cb, 1).astype(ml_dtypes.bfloat16)
    return cfa, cba


def _rot_tables():
    half = HD // 2
    inv = (np.float32(10000.0) ** (-np.arange(half, dtype=np.float32) / np.float32(half))).astype(np.float32)
    pos = np.concatenate([np.arange(T), PAST + np.arange(TS)]).astype(np.float32)
    ang = (pos[:, None] * inv[None, :]).astype(np.float32)
    return np.cos(ang).astype(np.float32), np.sin(ang).astype(np.float32)


def build_program():
    nc = bass.Bass("TRN2", target_bir_lowering=False)
    cfa, cba = _build_consts()
    NCF, NCB = cfa.shape[1], cba.shape[1]

    def din(name, shape, dt=F32):
        return nc.dram_tensor(name, list(shape), dt, kind="ExternalInput").ap()

    def dout(name, shape, dt=F32):
        return nc.dram_tensor(name, list(shape), dt, kind="ExternalOutput").ap()

    def dscr(name, shape, dt):
        return nc.dram_tensor(name, list(shape), dt).ap()

    I = dict(
        xT=din('xT', [D, T]), xsT=din('xsT', [NS, D, TS]),
        w_in=din('w_in', [DEPTH, D, NIN]), w_br=din('w_br', [DEPTH, 4, 512, D]), w_out=din('w_out', [DEPTH, D, D]),
        gcol=din('gcol', [128, DEPTH * 16]), fgcol=din('fgcol', [128, 16]),
        bif=din('bif', [128, DEPTH * 8]),
        convw=din('convw', [128, DEPTH * 4 * CW]), convb=din('convb', [128, DEPTH * 4]),
        lng=din('lng', [128, DEPTH * 4]), lnb=din('lnb', [128, DEPTH * 4]),
        cache_k=din('cache_k', [DEPTH, NS, PAST, 512]), cache_v=din('cache_v', [DEPTH, NS, PAST, 512]),
        stC=din('stC', [DEPTH, NS, H, 128, 128]), stn=din('stn', [128, DEPTH * NS * H]),
        stm=din('stm', [128, DEPTH * NS * H]), stS=din('stS', [DEPTH, NS, H, 128, 128]),
        stconv=din('stconv', [DEPTH, NS, 4, 128, CW - 1]),
        rcos=din('rcos', [T + TS, 64]), rsin=din('rsin', [T + TS, 64]),
        cf=din('cf', [128, NCF]), cb=din('cb', [128, NCB], BF),
    )
    O = dict(
        yT=dout('yT', [D, T]), ysT=dout('ysT', [NS, D, TS]),
        kT=dout('kT_o', [DEPTH, H, 128, T]), v=dout('v_o', [DEPTH, T, 512]),
        ksT=dout('ksT_o', [DEPTH, NS, H, 128, TS]), vs=dout('vs_o', [DEPTH, NS, TS, 512]),
        Cn=dout('Cn_o', [DEPTH, 1 + NS, H, 128, 129]), m=dout('m_o', [DEPTH, 1 + NS, H, 128, 1]),
        S=dout('S_o', [DEPTH, 1 + NS, H, 128, 128]), conv=dout('conv_o', [DEPTH, 1 + NS, 4, 128, CW - 1]),
    )
    X1 = dout('x1T', [D, T], F32) if NLAYER == 1 else dscr('x1T', [D, T], F32)
    X1s = dscr('x1sT', [NS, D, TS], F32)
    KSC = dscr('kscr', [DEPTH, H, 128, T], BF)
    VSC = dscr('vscr', [DEPTH, T, 512], BF)
    KSS = dscr('kscr_s', [DEPTH, NS, H, 128, PAST + TS], BF)
    VSS = dscr('vscr_s', [DEPTH, NS, PAST + TS, 512], BF)

    with ExitStack() as st:
        S = Sched(nc, st)

        def sb(name, shape, dt):
            return st.enter_context(nc.sbuf_tensor('sb_' + name, list(shape), dt))

        def pst(name, shape, dt):
            return st.enter_context(nc.psum_tensor('pp_' + name, list(shape), dt))

        def act(out, in_, func, r, w, scale=1.0, bias=None):
            if bias is None:
                S.op('act', lambda e: e.activation(out=out, in_=in_, func=func, scale=scale), r, w)
            else:
                S.op('act', lambda e: e.activation(out=out, in_=in_, func=func, scale=scale, bias=bias), r, w)

        def tt(out, a, b, op, r, w, eng='dve'):
            S.op(eng, lambda e: e.tensor_tensor(out=out, in0=a, in1=b, op=op), r, w)

        def ts(out, a, s1, op0, r, w, s2=None, op1=None, eng='dve'):
            if op1 is None:
                S.op(eng, lambda e: e.tensor_scalar(out=out, in0=a, scalar1=s1, scalar2=None, op0=op0), r, w)
            else:
                S.op(eng, lambda e: e.tensor_scalar(out=out, in0=a, scalar1=s1, scalar2=s2, op0=op0, op1=op1), r, w)

        def stt(out, a, s, b, op0, op1, r, w):
            S.op('dve', lambda e: e.scalar_tensor_tensor(out=out, in0=a, scalar=s, in1=b, op0=op0, op1=op1), r, w)

        def cp(out, in_, r, w, eng='dve'):
            S.op(eng, lambda e: e.tensor_copy(out=out, in_=in_), r, w)

        def mm(out, lhsT, rhs, start, stop, r, w):
            S.op('pe', lambda e: e.matmul(out, lhsT=lhsT, rhs=rhs, start=start, stop=stop), r, w)

        def tr(out, in_, ident, r, w):
            S.op('pe', lambda e: e.transpose(out, in_, ident), r, w)

        def memset(ap, val, w, eng='pool'):
            S.op(eng, lambda e: e.memset(ap, val), (), w)

        cf = sb('cf', [128, NCF], F32)
        cbt = sb('cbt', [128, NCB], BF)
        S.dma('sp', cf[:], I['cf'], (), ['cf'])
        S.dma('sp', cbt[:], I['cb'], (), ['cb'])

        def CF(name, rows=128, c0=0, c1=None):
            o, n = CF_LAYOUT[name]
            c1 = n if c1 is None else c1
            return cf[0:rows, o + c0:o + c1]

        def CB(name, rows=128, c0=0, c1=None):
            o, n = CB_LAYOUT[name]
            c1 = n if c1 is None else c1
            return cbt[0:rows, o + c0:o + c1]

        gcol = sb('gcol', [128, DEPTH * 16], F32)
        fgcol = sb('fgcol', [128, 16], F32)
        bif = sb('bif', [128, DEPTH * 8], F32)
        nbf = sb('nbf', [128, DEPTH * 8], F32)
        convw = sb('convw', [128, DEPTH * 4 * CW], F32)
        convb = sb('convb', [128, DEPTH * 4], F32)
        lng = sb('lng', [128, DEPTH * 4], F32)
        lnb = sb('lnb', [128, DEPTH * 4], F32)
        stn = sb('stn', [128, DEPTH * NS * H], F32)
        stm = sb('stm', [128, DEPTH * NS * H], F32)
        for nm, t_ in [('gcol', gcol), ('fgcol', fgcol), ('bif', bif), ('convw', convw), ('convb', convb),
                       ('lng', lng), ('lnb', lnb), ('stn', stn), ('stm', stm)]:
            S.dma('sp', t_[:], I[nm], (), ['par'])
        ts(nbf[:], bif[:], -1.0, ALU.mult, ['par'], ['par2'])

        NMAX = 512
        xs = [sb('xs%d' % i, [128, NMAX], F32) for i in range(2)]
        sqb = [sb('sqb%d' % i, [128, NMAX], BF) for i in range(2)]
        xb = sb('xb', [128, NDC, NMAX], BF)
        rstd_bc = sb('rstd_bc', [128, NMAX], F32)
        rstd_col = sb('rstd_col', [128, 4], F32)
        wb = [sb('wb%d' % i, [128, NDC, 512], BF) for i in range(2)]
        wsm = sb('wsm', [128, NDC, 8], BF)
        wbr = [sb('wbr%d' % i, [128, 4, 512], BF) for i in range(2)]
        merged = sb('merged', [128, NDC, NMAX], F32)
        mbf = xb
        yT = sb('yT', [128, 4, NMAX], BF)
        fmA = sb('fmA', [128, 4, NMAX], BF)
        fmB = sb('fmB', [128, 4, NMAX], BF)
        gate = sb('gate', [128, 4, NMAX], BF)
        f32t = [sb('f32t%d' % i, [128, NMAX], F32) for i in range(2)]
        tmA = sb('tmA', [128, 4, 4, 130], BF)
        tmB = sb('tmB', [128, 4, 512], BF)
        tmC = sb('tmC', [128, 4, 512], BF)
        tmF = [sb('tmF%d' % i, [128, 512], F32) for i in range(2)]
        ifc = sb('ifc', [128, 4, 8], F32)
        sig = [sb('sig%d' % i, [128, NMAX], F32) for i in range(2)]
        kblk = [sb('kblk%d' % i, [128, 512], BF) for i in range(2)]
        vblk = [sb('vblk%d' % i, [128, 4, 128], BF) for i in range(4)]
        a_e = [sb('a_e%d' % i, [128, NMAX], F32) for i in range(2)]
        a_sp = [sb('a_sp%d' % i, [128, NMAX], BF) for i in range(2)]
        a_x = [sb('a_x%d' % i, [128, NMAX], F32) for i in range(2)]
        a_w = [sb('a_w%d' % i, [128, NMAX], BF) for i in range(2)]
        carry = [sb('carry%d' % i, [128, NMAX], F32) for i in range(3)]
        Cext = [sb('Cext%d' % h, [128, 129], F32) for h in range(H)]
        Cbf = [sb('Cbf%d' % h, [128, 130], BF) for h in range(H)]
        Sst = [sb('Sst%d' % h, [128, 128], F32) for h in range(H)]
        Sbf = [sb('Sbf%d' % h, [128, 128], BF) for h in range(H)]
        Fc = sb('Fc', [128, H], F32)
        Mcc = sb('Mcc', [128, H], F32)
        Mcr = sb('Mcr', [1, H], F32)
        sm = sb('sm', [128, 64], F32)
        Mrow = sb('Mrow', [1, 128], F32)
        Dt = sb('Dt', [128, 128], F32)
        Wt = sb('Wt', [128, 128], BF)
        nd = sb('nd', [128, 130], F32)
        hh = sb('hh', [128, 128], F32)
        hn = sb('hn', [128, 128], BF)
        kw = sb('kw', [128, 128], BF)
        bnst = sb('bnst', [128, 6], F32)
        bnag = sb('bnag', [128, 2], F32)
        qkT = [sb('qkT%d' % i, [128, 128], BF) for i in range(2)]
        rot = [sb('rot%d' % i, [128, 128], F32) for i in range(2)]
        rcs = [sb('rcs%d' % i, [128, 128], F32) for i in range(2)]
        gext = [sb('gext%d' % c, [128, CW - 1 + NMAX], F32) for c in range(4)]
        gtmp = sb('gtmp', [128, CW - 1], F32)

        cv = [a_e[0], a_e[1], a_x[0], a_x[1]]
        CVK = ['a_e0', 'a_e1', 'a_x0', 'a_x1']
        for c_ in range(4):
            memset(tmA[:, c_, :, 128:129], 1.0, ['tmA%d' % c_])
        P = [pst('ps%d' % i, [128, 512], F32) for i in range(7)]
        PB = pst('psb', [128, 1024], BF)
        rr = {'proj': 0, 'z': 0}

        def pbank(kind):
            if kind == 'proj':
                rr['proj'] ^= 1
                return rr['proj'], 'P%d' % rr['proj']
            rr['z'] ^= 1
            return 2 + rr['z'], 'P%d' % (2 + rr['z'])

        wrr = {'ax': 0, 'v': 0, 'q2': 0, 'i': 0, 'b': 0, 'x': 0, 't': 0, 'k': 0, 'a': 0, 's': 0, 'q': 0, 'r': 0}

        def nxt(k, n=2):
            wrr[k] = (wrr[k] + 1) % n
            return wrr[k]

        def load_w(l, c0, ncols=512):
            if ncols == 8:
                S.dma('pool', wsm[:], I['w_in'][l, :, c0:c0 + 8].rearrange("(dc p) c -> p dc c", p=128), (), ['wsm'])
                return wsm, 'wsm'
            i = nxt('i')
            S.dma('pool', wb[i][:], I['w_in'][l, :, c0:c0 + ncols].rearrange("(dc p) c -> p dc c", p=128),
                  (), ['wb%d' % i])
            return wb[i], 'wb%d' % i

        def run_seq(l, q):
            N, L, nsb = q['N'], q['L'], q['nsb']
            nch = N // L
            last = (l == DEPTH - 1)
            go = l * 16

            for h in range(H):
                if q['init'] is None:
                    memset(Cext[h][:], 0.0, ['Cext%d' % h])
                    memset(Sst[h][:], 0.0, ['Sst%d' % h])
                else:
                    j = q['init']
                    S.dma('sp', Cext[h][:, 0:128], I['stC'][l, j, h], (), ['Cext%d' % h])
                    col = (l * NS + j) * H + h
                    cp(Cext[h][:, 128:129], stn[:, col:col + 1], ['par', 'Cext%d' % h], ['Cext%d' % h])
                    S.dma('sp', Sst[h][:], I['stS'][l, j, h], (), ['Sst%d' % h])
                cp(Cbf[h][:, 0:129], Cext[h][:], ['Cext%d' % h], ['Cbf%d' % h])
                cp(Sbf[h][:], Sst[h][:], ['Sst%d' % h], ['Sbf%d' % h])
            memset(Fc[:], 0.0, ['Fc'])
            if q['init'] is None:
                memset(Mcc[:], 0.0, ['Mcc'])
                memset(Mcr[:], 0.0, ['Mcr'])
            else:
                col = (l * NS + q['init']) * H
                cp(Mcc[:], stm[:, col:col + H], ['par'], ['Mcc'])
                cp(Mcr[:], stm[0:1, col:col + H], ['par'], ['Mcr'])
            for c in range(4):
                if q['init'] is None:
                    memset(gext[c][:, 0:CW - 1], 0.0, ['gext%d' % c])
                else:
                    S.dma('sp', gext[c][:, 0:CW - 1], I['stconv'][l, q['init'], c], (), ['gext%d' % c])

            xin = q['xin'][l]
            xout = q['xout'][l]

            for sbi in range(int(os.environ.get('MK_SB0', '0')) if q['init'] is None else 0, nsb):
                t0 = sbi * N
                pj, pk = pbank('proj')
                ssq = P[pj]
                for dc in range(NDC):
                    i = nxt('x')
                    S.dma('sp', xs[i][:, 0:N], xin[dc * 128:(dc + 1) * 128, t0:t0 + N], (), ['xs%d' % i])
                    act(sqb[i][:, 0:N], xs[i][:, 0:N], AF.Square, ['xs%d' % i], ['sqb%d' % i])
                    ts(xb[:, dc, 0:N], xs[i][:, 0:N], gcol[:, go + dc:go + dc + 1], ALU.mult,
                       ['xs%d' % i, 'par'], ['xb%d' % dc])
                    mm(ssq[:, 0:N], CB('onesb'), sqb[i][:, 0:N], dc == 0, dc == NDC - 1, ['cb', 'sqb%d' % i], [pk])
                    for c in range(nch):
                        mm(P[2 + c][0:L, 0:1], sqb[i][:, c * L:(c + 1) * L], CB('onesb', 128, 0, 1),
                           dc == 0, dc == NDC - 1, ['cb', 'sqb%d' % i], ['P%d' % (2 + c)])
                XB = ['xb%d' % dc for dc in range(NDC)]
                ts(rstd_bc[:, 0:N], ssq[:, 0:N], 1.0 / D, ALU.mult, [pk], ['rstd_bc'], s2=EPS, op1=ALU.add)
                act(rstd_bc[:, 0:N], rstd_bc[:, 0:N], AF.Sqrt, ['rstd_bc'], ['rstd_bc'])
                S.op('dve', lambda e: e.reciprocal(out=rstd_bc[:, 0:N], in_=rstd_bc[:, 0:N]), ['rstd_bc'], ['rstd_bc'])
                for c in range(nch):
                    ts(rstd_col[0:L, c:c + 1], P[2 + c][0:L, 0:1], 1.0 / D, ALU.mult, ['P%d' % (2 + c)], ['rstd_col'],
                       s2=EPS, op1=ALU.add)
                act(rstd_col[0:L, 0:nch], rstd_col[0:L, 0:nch], AF.Sqrt, ['rstd_col'], ['rstd_col'])
                S.op('dve', lambda e: e.reciprocal(out=rstd_col[0:L, 0:nch], in_=rstd_col[0:L, 0:nch]),
                     ['rstd_col'], ['rstd_col'])

                def fm_proj(w, wk, hcol):
                    pj, pk = pbank('proj')
                    for dc in range(NDC):
                        mm(P[pj][:, 0:N], w[:, dc, hcol:hcol + 128], xb[:, dc, 0:N], dc == 0, dc == NDC - 1,
                           [wk, 'xb%d' % dc], [pk])
                    return P[pj], pk

                def tm_proj(w, wk, c, ncols=512):
                    pj, pk = pbank('proj')
                    for dc in range(NDC):
                        mm(P[pj][0:L, 0:ncols], xb[:, dc, c * L:(c + 1) * L], w[:, dc, 0:ncols], dc == 0,
                           dc == NDC - 1, [wk, 'xb%d' % dc], [pk])
                    return P[pj], pk

                def gates_and_proj(br):
                    for g4 in range(4):
                        w, wk = load_w(l, OG + br * 2048 + g4 * 512)
                        bi = nxt('b')
                        S.dma('pool', wbr[bi][:], I['w_br'][l, br, :, g4 * 512:(g4 + 1) * 512]
                              .rearrange("(ec p) c -> p ec c", p=128), (), ['wbr%d' % bi])
                        for m4 in range(4):
                            mc = g4 * 4 + m4
                            ps, pk = fm_proj(w, wk, m4 * 128)
                            si = nxt('s')
                            tt(sig[si][:, 0:N], ps[:, 0:N], rstd_bc[:, 0:N], ALU.mult, [pk, 'rstd_bc'], ['sig%d' % si])
                            act(sig[si][:, 0:N], sig[si][:, 0:N], AF.Sigmoid, ['sig%d' % si], ['sig%d' % si])
                            pj, pk2 = pbank('proj')
                            for ec in range(4):
                                mm(P[pj][:, 0:N], wbr[bi][:, ec, m4 * 128:(m4 + 1) * 128], yT[:, ec, 0:N],
                                   ec == 0, ec == 3, ['wbr%d' % bi, 'yT%d' % ec], [pk2])
                            if br == 0:
                                tt(merged[:, mc, 0:N], P[pj][:, 0:N], sig[si][:, 0:N], ALU.mult,
                                   [pk2, 'sig%d' % si], ['mg%d' % mc])
                            else:
                                tt(sig[si][:, 0:N], P[pj][:, 0:N], sig[si][:, 0:N], ALU.mult,
                                   [pk2, 'sig%d' % si], ['sig%d' % si])
                                tt(merged[:, mc, 0:N], merged[:, mc, 0:N], sig[si][:, 0:N], ALU.add,
                                   ['mg%d' % mc, 'sig%d' % si], ['mg%d' % mc])

                def headnorm_to_yT(h, c, src, srck):
                    S.op('dve', lambda e: e.bn_stats(out=bnst[0:L, :], in_=src), [srck], ['bnst'])
                    S.op('dve', lambda e: e.bn_aggr(out=bnag[0:L, :], in_=bnst[0:L, :]), ['bnst'], ['bnag'])
                    ts(sm[0:L, 40:41], bnag[0:L, 1:2], LNEPS, ALU.add, ['bnag'], ['sm40'])
                    act(sm[0:L, 40:41], sm[0:L, 40:41], AF.Sqrt, ['sm40'], ['sm40'])
                    S.op('dve', lambda e: e.reciprocal(out=sm[0:L, 40:41], in_=sm[0:L, 40:41]), ['sm40'], ['sm40'])
                    ts(hn[0:L, :], src, bnag[0:L, 0:1], ALU.subtract, [srck, 'bnag', 'sm40'], ['hn'],
                       s2=sm[0:L, 40:41], op1=ALU.mult)
                    tr(PB[:, 0:L], hn[0:L, :], CB('identb', L, 0, L), ['hn', 'cb'], ['PB'])
                    tt(yT[:, h, c * L:(c + 1) * L], PB[:, 0:L], gate[:, h, c * L:(c + 1) * L], ALU.mult,
                       ['PB', 'gate%d' % h], ['yT%d' % h])

                w, wk = load_w(l, OA)
                for h in range(H):
                    ps, pk = fm_proj(w, wk, h * 128)
                    stt(fmA[:, h, 0:N], ps[:, 0:N], SC, rstd_bc[:, 0:N], ALU.mult, ALU.mult, [pk, 'rstd_bc'], ['fmA%d' % h])
                w, wk = load_w(l, OA + 512)
                for h in range(H):
                    ps, pk = fm_proj(w, wk, h * 128)
                    fi = nxt('t')
                    tt(f32t[fi][:, 0:N], ps[:, 0:N], rstd_bc[:, 0:N], ALU.mult, [pk, 'rstd_bc'], ['f32t%d' % fi])
                    S.dma('sp', q['k_out'][l][h][:, t0:t0 + N], f32t[fi][:, 0:N], ['f32t%d' % fi], ['kout'])
                    cp(fmB[:, h, 0:N], f32t[fi][:, 0:N], ['f32t%d' % fi], ['fmB%d' % h])
                    S.dma('sp', q['kscr'][l][h][:, q['kbase'] + t0:q['kbase'] + t0 + N], fmB[:, h, 0:N],
                          ['fmB%d' % h], ['kscr'])
                w, wk = load_w(l, OA + 1024)
                for c in range(nch):
                    ps, pk = tm_proj(w, wk, c)
                    fi = nxt('r')
                    ts(tmF[fi][0:L, :], ps[0:L, :], rstd_col[0:L, c:c + 1], ALU.mult, [pk, 'rstd_col'], ['tmF%d' % fi])
                    S.dma('sp', q['v_out'][l][t0 + c * L:t0 + (c + 1) * L, :], tmF[fi][0:L, :], ['tmF%d' % fi], ['vout'])
                    cp(tmB[0:L, c, :], tmF[fi][0:L, :], ['tmF%d' % fi], ['tmB%d' % c])
                    S.dma('sp', q['vscr'][l][q['kbase'] + t0 + c * L:q['kbase'] + t0 + (c + 1) * L, :], tmB[0:L, c, :],
                          ['tmB%d' % c], ['vscr'])
                w, wk = load_w(l, OA + 1536)
                for h in range(H):
                    ps, pk = fm_proj(w, wk, h * 128)
                    fi = nxt('t')
                    tt(f32t[fi][:, 0:N], ps[:, 0:N], rstd_bc[:, 0:N], ALU.mult, [pk, 'rstd_bc'], ['f32t%d' % fi])
                    act(gate[:, h, 0:N], f32t[fi][:, 0:N], AF.Silu, ['f32t%d' % fi], ['gate%d' % h])

                nkeys = q['kbase'] + t0 + N
                blocks = []
                kpos = nkeys
                while kpos > 0:
                    ks = min(128, kpos - ((kpos - 1) // 128) * 128)
                    blocks.append((kpos - ks, ks))
                    kpos -= ks
                if os.environ.get('MK_NBLK'):
                    blocks = blocks[:int(os.environ['MK_NBLK'])]
                if os.environ.get('MK_DIAGONLY'):
                    blocks = [b_ for b_ in blocks if b_[0] >= q['kbase'] + t0]
                ZB = [2, 3, 0]
                for h in range(H):
                    stt_ = {'first': True}
                    nblk = len(blocks)

                    def s1a(bi_, k0, ks, h=h):
                        diag = (k0 >= q['kbase'] + t0)
                        jd = (k0 - q['kbase'] - t0) // 128 if diag else -1
                        kk = nxt('k')
                        vk = nxt('v', 4)
                        S.dma('sp', kblk[kk][:, 0:ks], q['kscr'][l][h][:, k0:k0 + ks], ['kscr'], ['kblk%d' % kk])
                        S.dma('sp', vblk[vk][0:ks, 0, :], q['vscr'][l][k0:k0 + ks, h * 128:(h + 1) * 128], ['vscr'],
                              ['vblk%d' % vk])
                        zj = ZB[bi_ % 3]
                        zk = 'P%d' % zj
                        Z = P[zj]
                        mm(Z[0:ks, 0:N], kblk[kk][:, 0:ks], fmA[:, h, 0:N], True, True,
                           ['kblk%d' % kk, 'fmA%d' % h], [zk])
                        return dict(bi=bi_, ks=ks, diag=diag, jd=jd, vk=vk, Z=Z, zk=zk)

                    def s1b(t):
                        bi_, ks, diag, jd, Z, zk = t['bi'], t['ks'], t['diag'], t['jd'], t['Z'], t['zk']
                        ai = nxt('a')
                        t['ai'] = ai
                        act(a_e[ai][0:ks, 0:N], Z[0:ks, 0:N], AF.Exp, [zk], ['a_e%d' % ai])
                        act(a_sp[ai][0:ks, 0:N], a_e[ai][0:ks, 0:N], AF.Ln, ['a_e%d' % ai], ['a_sp%d' % ai], bias=1.0)
                    def s1c(t):
                        bi_, ks, diag, jd, Z, zk, ai = t['bi'], t['ks'], t['diag'], t['jd'], t['Z'], t['zk'], t['ai']
                        if diag:
                            tt(a_sp[ai][0:ks, 0:N], a_sp[ai][0:ks, 0:N], CB('m01_%d' % jd, ks, 0, N), ALU.mult,
                               ['a_sp%d' % ai, 'cb'], ['a_sp%d' % ai])
                        else:
                            ts(a_sp[ai][0:ks, 0:N], a_sp[ai][0:ks, 0:N], 1.0, ALU.mult, ['a_sp%d' % ai], ['a_sp%d' % ai])
                        p4i = 4 if bi_ % 2 == 0 else 1
                        t['p4'] = p4i
                        mm(P[p4i][0:ks, 0:N], CB('neguincl', ks, 0, ks), a_sp[ai][0:ks, 0:N], True, True,
                           ['cb', 'a_sp%d' % ai], ['P%d' % p4i])
                        if bi_ < nblk - 1:
                            mm(P[5][:, 0:N], CB('negonesb', ks, 0, 128), a_sp[ai][0:ks, 0:N], True, True,
                               ['cb', 'a_sp%d' % ai], ['P5'])

                    def s1d(t):
                        bi_ = t['bi']
                        if bi_ < nblk - 1:
                            cn = (bi_ + 1) % 3
                            if bi_ == 0:
                                cp(carry[cn][:, 0:N], P[5][:, 0:N], ['P5'], ['carry%d' % cn])
                            else:
                                tt(carry[cn][:, 0:N], carry[bi_ % 3][:, 0:N], P[5][:, 0:N], ALU.add,
                                   ['carry%d' % (bi_ % 3), 'P5'], ['carry%d' % cn])

                    def s2(t, h=h):
                        bi_, ks, diag, jd, vk, Z, zk, ai = t['bi'], t['ks'], t['diag'], t['jd'], t['vk'], t['Z'], t['zk'], t['ai']
                        P4, k4 = P[t['p4']], 'P%d' % t['p4']
                        first = stt_['first']
                        ci = bi_ % 3
                        xi = nxt('ax')
                        if first:
                            assert diag
                            tt(a_x[xi][0:ks, 0:N], Z[0:ks, 0:N], CB('negm%d' % jd, ks, 0, N), ALU.add, [zk, 'cb'],
                               ['a_x%d' % xi])
                        else:
                            tt(a_x[xi][0:ks, 0:N], Z[0:ks, 0:N], carry[ci][0:ks, 0:N], ALU.add,
                               [zk, 'carry%d' % ci], ['a_x%d' % xi])
                            if diag:
                                tt(a_x[xi][0:ks, 0:N], a_x[xi][0:ks, 0:N], CB('negm%d' % jd, ks, 0, N), ALU.add,
                                   ['a_x%d' % xi, 'cb'], ['a_x%d' % xi])
                        tt(a_x[xi][0:ks, 0:N], a_x[xi][0:ks, 0:N], P4[0:ks, 0:N], ALU.add, ['a_x%d' % xi, k4],
                           ['a_x%d' % xi])
                        t['xi'] = xi
                        t['first'] = first
                        stt_['first'] = False

                    def s2b(t):
                        ks, xi = t['ks'], t['xi']
                        act(a_w[xi][0:ks, 0:N], a_x[xi][0:ks, 0:N], AF.Exp, ['a_x%d' % xi], ['a_w%d' % xi])

                    def s2c(t):
                        bi_, ks, vk, xi = t['bi'], t['ks'], t['vk'], t['xi']
                        mm(P[6][:, 0:N], vblk[vk][0:ks, 0, :], a_w[xi][0:ks, 0:N], t['first'], bi_ == nblk - 1,
                           ['vblk%d' % vk, 'a_w%d' % xi], ['P6'])

                    tl = {}
                    for it_ in range(nblk + 2):
                        b2 = it_ - 2
                        b1 = it_ - 1
                        if it_ < nblk:
                            tl[it_] = s1a(it_, blocks[it_][0], blocks[it_][1])
                        if 0 <= b1 < nblk:
                            s1b(tl[b1])
                        if 0 <= b2 < nblk:
                            s2(tl[b2])
                        if 0 <= b1 < nblk:
                            s1c(tl[b1])
                        if 0 <= b2 < nblk:
                            s2b(tl[b2])
                        if 0 <= b1 < nblk:
                            s1d(tl[b1])
                        if 0 <= b2 < nblk:
                            s2c(tl.pop(b2))
                    tt(yT[:, h, 0:N], P[6][:, 0:N], gate[:, h, 0:N], ALU.mult, ['P6', 'gate%d' % h], ['yT%d' % h])
                gates_and_proj(0)

                w, wk = load_w(l, OB)
                for h in range(H):
                    ps, pk = fm_proj(w, wk, h * 128)
                    tt(fmA[:, h, 0:N], ps[:, 0:N], rstd_bc[:, 0:N], ALU.mult, [pk, 'rstd_bc'], ['fmA%d' % h])
                w, wk = load_w(l, OB + 512)
                for h in range(H):
                    ps, pk = fm_proj(w, wk, h * 128)
                    tt(fmB[:, h, 0:N], ps[:, 0:N], rstd_bc[:, 0:N], ALU.mult, [pk, 'rstd_bc'], ['fmB%d' % h])
                for c in range(nch):
                    ps, pk = tm_proj(w, wk, c)
                    ts(tmB[0:L, c, :], ps[0:L, :], rstd_col[0:L, c:c + 1], ALU.mult, [pk, 'rstd_col'], ['tmB%d' % c])
                w, wk = load_w(l, OB + 1024)
                for c in range(nch):
                    ps, pk = tm_proj(w, wk, c)
                    for h in range(H):
                        ts(tmA[0:L, c, h, 0:128], ps[0:L, h * 128:(h + 1) * 128], rstd_col[0:L, c:c + 1], ALU.mult,
                           [pk, 'rstd_col'], ['tmA%d' % c])
                w, wk = load_w(l, OB + 1536)
                for h in range(H):
                    ps, pk = fm_proj(w, wk, h * 128)
                    fi = nxt('t')
                    tt(f32t[fi][:, 0:N], ps[:, 0:N], rstd_bc[:, 0:N], ALU.mult, [pk, 'rstd_bc'], ['f32t%d' % fi])
                    act(gate[:, h, 0:N], f32t[fi][:, 0:N], AF.Sigmoid, ['f32t%d' % fi], ['gate%d' % h])
                w, wk = load_w(l, OB + 2048)
                for h in range(H):
                    ps, pk = fm_proj(w, wk, h * 128)
                    fi = nxt('t')
                    tt(f32t[fi][:, 0:N], ps[:, 0:N], rstd_bc[:, 0:N], ALU.mult, [pk, 'rstd_bc'], ['f32t%d' % fi])
                    act(f32t[fi][:, 0:N], f32t[fi][:, 0:N], AF.Silu, ['f32t%d' % fi], ['f32t%d' % fi])
                    tt(gate[:, h, 0:N], gate[:, h, 0:N], f32t[fi][:, 0:N], ALU.mult, ['gate%d' % h, 'f32t%d' % fi],
                       ['gate%d' % h])
                w, wk = load_w(l, OIF, 8)
                for c in range(nch):
                    ps, pk = tm_proj(w, wk, c, 8)
                    ts(ifc[0:L, c, :], ps[0:L, 0:8], rstd_col[0:L, c:c + 1], ALU.mult, [pk, 'rstd_col'], ['ifc%d' % c])
                for c in range(nch):
                    for h in range(H):
                        bo = l * 8
                        A_ = P[2]
                        act(sm[0:L, 0:1], ifc[0:L, c, 4 + h:5 + h], AF.Exp, ['ifc%d' % c, 'par2'], ['sm0'], scale=-1.0,
                            bias=nbf[0:L, bo + 4 + h:bo + 5 + h])
                        act(sm[0:L, 1:2], sm[0:L, 0:1], AF.Ln, ['sm0'], ['sm1'], bias=1.0)
                        mm(A_[0:L, 0:1], CF('negtri', L, 0, L), sm[0:L, 1:2], True, True, ['cf', 'sm1'], ['P2'])
                        mm(A_[:, 1:2], CF('negones', L, 0, 128), sm[0:L, 1:2], True, True, ['cf', 'sm1'], ['P2'])
                        tt(sm[0:L, 2:3], A_[0:L, 0:1], Fc[0:L, h:h + 1], ALU.add, ['P2', 'Fc'], ['sm2'])
                        tt(Fc[:, h:h + 1], Fc[:, h:h + 1], A_[:, 1:2], ALU.add, ['Fc', 'P2', 'sm2'], ['Fc'])
                        stt(sm[0:L, 3:4], ifc[0:L, c, h:h + 1], bif[0:L, bo + h:bo + h + 1], sm[0:L, 2:3], ALU.add,
                            ALU.subtract, ['ifc%d' % c, 'par', 'sm2'], ['sm3'])
                        mm(A_[0:1, 16:16 + L], sm[0:L, 3:4], CF('ident', L, 0, L), True, True, ['sm3', 'cf'], ['P2'])
                        S.op('dve', lambda e, h=h: e.tensor_tensor_scan(out=Mrow[0:1, 0:L], data0=CF('zeros', 1, 0, L),
                                                                         data1=A_[0:1, 16:16 + L],
                                                                         initial=Mcr[0:1, h:h + 1], op0=ALU.add,
                                                                         op1=ALU.max),
                             ['P2', 'cf', 'Mcr'], ['Mrow'])
                        mm(P[3][0:L, 0:L], CF('ones', 1, 0, L), Mrow[0:1, 0:L], True, True, ['cf', 'Mrow'], ['P3'])
                        mm(A_[0:L, 2:3], Mrow[0:1, 0:L], CF('ones', 1, 0, 1), True, True, ['cf', 'Mrow'], ['P2'])
                        mm(A_[:, 3:4], CF('ones', 1, 0, 128), Mrow[0:1, L - 1:L], True, True, ['cf', 'Mrow'], ['P2'])
                        cp(Mcr[0:1, h:h + 1], Mrow[0:1, L - 1:L], ['Mrow'], ['Mcr'])
                        act(Dt[0:L, 0:L], P[3][0:L, 0:L], AF.Exp, ['P3', 'sm3'], ['Dt'], scale=-1.0, bias=sm[0:L, 3:4])
                        mm(P[3][0:L, 128:128 + L], fmB[:, h, c * L:(c + 1) * L], fmA[:, h, c * L:(c + 1) * L], True, True,
                           ['fmB%d' % h, 'fmA%d' % h], ['P3'])
                        tt(Dt[0:L, 0:L], Dt[0:L, 0:L], CF('masksc', L, 0, L), ALU.mult, ['Dt', 'cf'], ['Dt'])
                        tt(Wt[0:L, 0:L], Dt[0:L, 0:L], P[3][0:L, 128:128 + L], ALU.mult, ['Dt', 'P3'], ['Wt'])
                        mm(P[4][0:L, 0:129], Wt[0:L, 0:L], tmA[0:L, c, h, 0:129], True, True, ['Wt', 'tmA%d' % c], ['P4'])
                        mm(P[5][0:L, 0:129], fmA[:, h, c * L:(c + 1) * L], Cbf[h][:, 0:129], True, True,
                           ['fmA%d' % h, 'Cbf%d' % h], ['P5'])
                        act(sm[0:L, 4:5], A_[0:L, 2:3], AF.Exp, ['P2', 'Mcc'], ['sm4'], scale=-1.0,
                            bias=Mcc[0:L, h:h + 1])
                        act(nd[0:L, 0:129], P[5][0:L, 0:129], AF.Copy, ['P5', 'sm4'], ['nd'], scale=sm[0:L, 4:5])
                        tt(nd[0:L, 0:129], nd[0:L, 0:129], P[4][0:L, 0:129], ALU.add, ['nd', 'P4'], ['nd'])
                        tt(sm[0:L, 5:6], sm[0:L, 2:3], A_[0:L, 2:3], ALU.add, ['sm2', 'P2'], ['sm5'])
                        act(sm[0:L, 6:7], sm[0:L, 5:6], AF.Exp, ['sm5'], ['sm6'], scale=-1.0)
                        ts(sm[0:L, 7:8], nd[0:L, 128:129], -1.0, ALU.mult, ['nd'], ['sm7'])
                        tt(sm[0:L, 7:8], sm[0:L, 7:8], nd[0:L, 128:129], ALU.max, ['nd', 'sm7'], ['sm7'])
                        tt(sm[0:L, 7:8], sm[0:L, 7:8], sm[0:L, 6:7], ALU.max, ['sm6', 'sm7'], ['sm7'])
                        S.op('dve', lambda e: e.reciprocal(out=sm[0:L, 7:8], in_=sm[0:L, 7:8]), ['sm7'], ['sm7'])
                        ts(hh[0:L, :], nd[0:L, 0:128], sm[0:L, 7:8], ALU.mult, ['nd', 'sm7'], ['hh'])
                        headnorm_to_yT(h, c, hh[0:L, :], 'hh')
                        ts(sm[:, 8:9], A_[:, 3:4], -1.0, ALU.mult, ['P2'], ['sm8'])
                        act(sm[0:L, 9:10], sm[0:L, 3:4], AF.Exp, ['sm3', 'sm8'], ['sm9'], bias=sm[0:L, 8:9])
                        ts(kw[0:L, :], tmB[0:L, c, h * 128:(h + 1) * 128], sm[0:L, 9:10], ALU.mult, ['tmB%d' % c, 'sm9'],
                           ['kw'], s2=SC, op1=ALU.mult)
                        mm(P[4][:, 256:385], kw[0:L, :], tmA[0:L, c, h, 0:129], True, True, ['kw', 'tmA%d' % c], ['P4'])
                        act(sm[:, 10:11], Mcc[:, h:h + 1], AF.Exp, ['Mcc', 'sm8'], ['sm10'], bias=sm[:, 8:9])
                        stt(Cext[h][:], Cext[h][:], sm[:, 10:11], P[4][:, 256:385], ALU.mult, ALU.add,
                            ['Cext%d' % h, 'sm10', 'P4'], ['Cext%d' % h])
                        cp(Cbf[h][:, 0:129], Cext[h][:], ['Cext%d' % h], ['Cbf%d' % h])
                        ts(Mcc[:, h:h + 1], sm[:, 8:9], -1.0, ALU.mult, ['sm8', 'sm4', 'sm10'], ['Mcc'])
                gates_and_proj(1)

                lg = _gammas()
                for which, off, dst in (('q', OC, tmC), ('k', OC + 512, tmB)):
                    w, wk = load_w(l, off)
                    for c in range(nch):
                        ps, pk = tm_proj(w, wk, c)
                        ri = nxt('q')
                        r0 = q['pos0'] + t0 + c * L
                        S.dma('sp', rcs[ri][0:L, 0:64], I['rcos'][r0:r0 + L, :], (), ['rcs%d' % ri])
                        S.dma('sp', rcs[ri][0:L, 64:128], I['rsin'][r0:r0 + L, :], (), ['rcs%d' % ri])
                        fi = nxt('r')
                        ts(tmF[fi][0:L, :], ps[0:L, :], rstd_col[0:L, c:c + 1], ALU.mult, [pk, 'rstd_col'], ['tmF%d' % fi])
                        for h in range(H):
                            x1 = tmF[fi][0:L, h * 128:h * 128 + 64]
                            x2 = tmF[fi][0:L, h * 128 + 64:h * 128 + 128]
                            cs, sn = rcs[ri][0:L, 0:64], rcs[ri][0:L, 64:128]
                            ri2 = nxt('q2')
                            R = rot[ri2]
                            rk = 'rot%d' % ri2
                            tt(R[0:L, 0:64], x1, cs, ALU.mult, ['tmF%d' % fi, 'rcs%d' % ri], [rk])
                            tt(R[0:L, 64:128], x2, sn, ALU.mult, ['tmF%d' % fi, 'rcs%d' % ri], [rk])
                            tt(R[0:L, 0:64], R[0:L, 0:64], R[0:L, 64:128], ALU.subtract, [rk], [rk])
                            tt(R[0:L, 64:128], x1, sn, ALU.mult, ['tmF%d' % fi, 'rcs%d' % ri, rk], [rk])
                            tt(hh[0:L, 0:64], x2, cs, ALU.mult, ['tmF%d' % fi, 'rcs%d' % ri], ['hh'])
                            tt(R[0:L, 64:128], R[0:L, 64:128], hh[0:L, 0:64], ALU.add, [rk, 'hh'], [rk])
                            dec = CF('qdec' if which == 'q' else 'kdec', L, h, h + 1)
                            ts(dst[0:L, c, h * 128:(h + 1) * 128], R[0:L, :], dec, ALU.mult, [rk, 'cf'],
                               ['%s%d' % ('tmC' if which == 'q' else 'tmB', c)])
                w, wk = load_w(l, OC + 1024)
                for c in range(nch):
                    ps, pk = tm_proj(w, wk, c)
                    for h in range(H):
                        ts(tmA[0:L, c, h, 0:128], ps[0:L, h * 128:(h + 1) * 128], rstd_col[0:L, c:c + 1], ALU.mult,
                           [pk, 'rstd_col'], ['tmA%d' % c])
                w, wk = load_w(l, OC + 1536)
                for h in range(H):
                    ps, pk = fm_proj(w, wk, h * 128)
                    fi = nxt('t')
                    tt(f32t[fi][:, 0:N], ps[:, 0:N], rstd_bc[:, 0:N], ALU.mult, [pk, 'rstd_bc'], ['f32t%d' % fi])
                    act(gate[:, h, 0:N], f32t[fi][:, 0:N], AF.Silu, ['f32t%d' % fi], ['gate%d' % h])
                for c in range(nch):
                    for h in range(H):
                        gL = float(np.exp(lg[h] * L))
                        tr(PB[:, 0:L], tmC[0:L, c, h * 128:(h + 1) * 128], CB('identb', L, 0, L), ['tmC%d' % c, 'cb'], ['PB'])
                        cp(qkT[0][:, 0:L], PB[:, 0:L], ['PB'], ['qkT0'])
                        tr(PB[:, 512:512 + L], tmB[0:L, c, h * 128:(h + 1) * 128], CB('identb', L, 0, L),
                           ['tmB%d' % c, 'cb'], ['PB'])
                        act(qkT[1][:, 0:L], PB[:, 512:512 + L], AF.Copy, ['PB'], ['qkT1'])
                        mm(P[3][0:L, 0:L], qkT[1][:, 0:L], qkT[0][:, 0:L], True, True, ['qkT0', 'qkT1'], ['P3'])
                        tt(Wt[0:L, 0:L], P[3][0:L, 0:L], CF('mask01', L, 0, L), ALU.mult, ['P3', 'cf'], ['Wt'])
                        mm(P[4][0:L, 0:128], Wt[0:L, 0:L], tmA[0:L, c, h, 0:128], True, False, ['Wt', 'tmA%d' % c], ['P4'])
                        mm(P[4][0:L, 0:128], qkT[0][:, 0:L], Sbf[h][:], False, True, ['qkT0', 'Sbf%d' % h], ['P4'])
                        cp(hh[0:L, :], P[4][0:L, 0:128], ['P4'], ['hh'])
                        headnorm_to_yT(h, c, hh[0:L, :], 'hh')
                        mm(P[4][:, 256:384], tmB[0:L, c, h * 128:(h + 1) * 128], tmA[0:L, c, h, 0:128], True, True,
                           ['tmB%d' % c, 'tmA%d' % c], ['P4'])
                        tt(Sst[h][:], Sst[h][:], P[4][:, 256:384], ALU.add, ['Sst%d' % h, 'P4'], ['Sst%d' % h])
                        ts(Sst[h][:], Sst[h][:], gL, ALU.mult, ['Sst%d' % h], ['Sst%d' % h])
                        cp(Sbf[h][:], Sst[h][:], ['Sst%d' % h], ['Sbf%d' % h])
                gates_and_proj(2)

                w, wk = load_w(l, OD)
                w2, wk2 = load_w(l, OD + 512)
                for c4 in range(4):
                    ps, pk = fm_proj(w, wk, c4 * 128)
                    tt(gext[c4][:, CW - 1:CW - 1 + N], ps[:, 0:N], rstd_bc[:, 0:N], ALU.mult, [pk, 'rstd_bc'],
                       ['gext%d' % c4])
                    ps, pk = fm_proj(w2, wk2, c4 * 128)
                    fi = nxt('t')
                    tt(f32t[fi][:, 0:N], ps[:, 0:N], rstd_bc[:, 0:N], ALU.mult, [pk, 'rstd_bc'], ['f32t%d' % fi])
                    act(f32t[fi][:, 0:N], f32t[fi][:, 0:N], AF.Sigmoid, ['f32t%d' % fi], ['f32t%d' % fi])
                    tt(gext[c4][:, CW - 1:CW - 1 + N], gext[c4][:, CW - 1:CW - 1 + N], f32t[fi][:, 0:N], ALU.mult,
                       ['gext%d' % c4, 'f32t%d' % fi], ['gext%d' % c4])
                w, wk = load_w(l, OD + 1024)
                for c4 in range(4):
                    ps, pk = fm_proj(w, wk, c4 * 128)
                    fi = nxt('t')
                    tt(f32t[fi][:, 0:N], ps[:, 0:N], rstd_bc[:, 0:N], ALU.mult, [pk, 'rstd_bc'], ['f32t%d' % fi])
                    act(fmB[:, c4, 0:N], f32t[fi][:, 0:N], AF.Silu, ['f32t%d' % fi], ['fmB%d' % c4])
                for c4 in range(4):
                    wo = (l * 4 + c4) * CW
                    ts(cv[c4][:, 0:N], gext[c4][:, 0:N], convw[:, wo:wo + 1], ALU.mult, ['gext%d' % c4, 'par'],
                       [CVK[c4]], s2=convb[:, l * 4 + c4:l * 4 + c4 + 1], op1=ALU.add)
                    for k in range(1, CW):
                        stt(cv[c4][:, 0:N], gext[c4][:, k:k + N], convw[:, wo + k:wo + k + 1], cv[c4][:, 0:N],
                            ALU.mult, ALU.add, ['gext%d' % c4, 'par', CVK[c4]], [CVK[c4]])
                    cp(gtmp[:], gext[c4][:, N:N + CW - 1], ['gext%d' % c4], ['gtmp'])
                    cp(gext[c4][:, 0:CW - 1], gtmp[:], ['gtmp'], ['gext%d' % c4])
                for c4 in range(4):
                    mm(P[2][:, 0:N], CF('o512'), cv[c4][:, 0:N], c4 == 0, c4 == 3, ['cf', CVK[c4]], ['P2'])
                for c4 in range(4):
                    fi = nxt('t')
                    act(f32t[fi][:, 0:N], cv[c4][:, 0:N], AF.Square, [CVK[c4]], ['f32t%d' % fi])
                    mm(P[3][:, 0:N], CF('o512'), f32t[fi][:, 0:N], c4 == 0, c4 == 3, ['cf', 'f32t%d' % fi], ['P3'])
                m_ = sig[0]
                r_ = sig[1]
                cp(m_[:, 0:N], P[2][:, 0:N], ['P2'], ['sig0'])
                tt(r_[:, 0:N], m_[:, 0:N], m_[:, 0:N], ALU.mult, ['sig0'], ['sig1'])
                tt(r_[:, 0:N], P[3][:, 0:N], r_[:, 0:N], ALU.subtract, ['P3', 'sig1'], ['sig1'])
                ts(r_[:, 0:N], r_[:, 0:N], LNEPS, ALU.add, ['sig1'], ['sig1'])
                act(r_[:, 0:N], r_[:, 0:N], AF.Sqrt, ['sig1'], ['sig1'])
                S.op('dve', lambda e: e.reciprocal(out=r_[:, 0:N], in_=r_[:, 0:N]), ['sig1'], ['sig1'])
                for c4 in range(4):
                    fi = nxt('t')
                    tt(f32t[fi][:, 0:N], cv[c4][:, 0:N], m_[:, 0:N], ALU.subtract, [CVK[c4], 'sig0'], ['f32t%d' % fi])
                    tt(f32t[fi][:, 0:N], f32t[fi][:, 0:N], r_[:, 0:N], ALU.mult, ['f32t%d' % fi, 'sig1'], ['f32t%d' % fi])
                    ts(f32t[fi][:, 0:N], f32t[fi][:, 0:N], lng[:, l * 4 + c4:l * 4 + c4 + 1], ALU.mult,
                       ['f32t%d' % fi, 'par'], ['f32t%d' % fi], s2=lnb[:, l * 4 + c4:l * 4 + c4 + 1], op1=ALU.add)
                    act(f32t[fi][:, 0:N], f32t[fi][:, 0:N], AF.Silu, ['f32t%d' % fi], ['f32t%d' % fi])
                    tt(yT[:, c4, 0:N], f32t[fi][:, 0:N], fmB[:, c4, 0:N], ALU.mult, ['f32t%d' % fi, 'fmB%d' % c4],
                       ['yT%d' % c4])
                gates_and_proj(3)

                for mc in range(NDC):
                    cp(mbf[:, mc, 0:N], merged[:, mc, 0:N], ['mg%d' % mc], ['xb%d' % mc])
                for g4 in range(4):
                    i = nxt('i')
                    S.dma('pool', wb[i][:], I['w_out'][l, :, g4 * 512:(g4 + 1) * 512].rearrange("(dc p) c -> p dc c", p=128),
                          (), ['wb%d' % i])
                    for m4 in range(4):
                        mc = g4 * 4 + m4
                        pj, pk = pbank('proj')
                        for m2 in range(NDC):
                            mm(P[pj][:, 0:N], wb[i][:, m2, m4 * 128:(m4 + 1) * 128], mbf[:, m2, 0:N], m2 == 0,
                               m2 == NDC - 1, ['wb%d' % i, 'xb%d' % m2], [pk])
                        xi = nxt('x')
                        S.dma('sp', xs[xi][:, 0:N], xin[mc * 128:(mc + 1) * 128, t0:t0 + N], (), ['xs%d' % xi])
                        tt(merged[:, mc, 0:N], P[pj][:, 0:N], xs[xi][:, 0:N], ALU.add,
                           [pk, 'xs%d' % xi], ['mg%d' % mc])
                        if not last:
                            S.dma('sp', xout[mc * 128:(mc + 1) * 128, t0:t0 + N], merged[:, mc, 0:N], ['mg%d' % mc], ['xout'])
                if last:
                    pj, pk = pbank('proj')
                    for mc in range(NDC):
                        si = nxt('x')
                        act(sqb[si][:, 0:N], merged[:, mc, 0:N], AF.Square, ['mg%d' % mc], ['sqb%d' % si])
                        mm(P[pj][:, 0:N], CB('onesb'), sqb[si][:, 0:N], mc == 0, mc == NDC - 1, ['cb', 'sqb%d' % si], [pk])
                    ts(rstd_bc[:, 0:N], P[pj][:, 0:N], 1.0 / D, ALU.mult, [pk], ['rstd_bc'], s2=EPS, op1=ALU.add)
                    act(rstd_bc[:, 0:N], rstd_bc[:, 0:N], AF.Sqrt, ['rstd_bc'], ['rstd_bc'])
                    S.op('dve', lambda e: e.reciprocal(out=rstd_bc[:, 0:N], in_=rstd_bc[:, 0:N]), ['rstd_bc'], ['rstd_bc'])
                    for mc in range(NDC):
                        stt(merged[:, mc, 0:N], merged[:, mc, 0:N], fgcol[:, mc:mc + 1], rstd_bc[:, 0:N], ALU.mult,
                            ALU.mult, ['mg%d' % mc, 'par', 'rstd_bc'], ['mg%d' % mc])
                        S.dma('sp', xout[mc * 128:(mc + 1) * 128, t0:t0 + N], merged[:, mc, 0:N], ['mg%d' % mc], ['xout'])

            si_ = q['sidx']
            for h in range(H):
                S.dma('sp', O['Cn'][l, si_, h], Cext[h][:], ['Cext%d' % h], ['o_Cn'])
                S.dma('sp', O['S'][l, si_, h], Sst[h][:], ['Sst%d' % h], ['o_S'])
                tt(sm[:, 20 + h:21 + h], Fc[:, h:h + 1], Mcc[:, h:h + 1], ALU.add, ['Fc', 'Mcc'], ['sm2%d' % h])
                S.dma('sp', O['m'][l, si_, h], sm[:, 20 + h:21 + h], ['sm2%d' % h], ['o_m'])
            for c4 in range(4):
                S.dma('sp', O['conv'][l, si_, c4], gext[c4][:, 0:CW - 1], ['gext%d' % c4], ['o_conv'])

        def prep_sample(l, j):
            for kb in range(PAST // 128):
                kk = nxt('k')
                S.dma('pool', vblk[kk][:].rearrange("p a b -> p (a b)"), I['cache_k'][l, j, kb * 128:(kb + 1) * 128, :],
                      (), ['vblk%d' % kk])
                for h in range(H):
                    tr(PB[:, h * 128:(h + 1) * 128], vblk[kk][:, h, :], CB('identb'), ['vblk%d' % kk, 'cb'], ['PB'])
                ai = nxt('a')
                cp(a_sp[ai][:, 0:512], PB[:, 0:512], ['PB'], ['a_sp%d' % ai])
                for h in range(H):
                    S.dma('sp', KSS[l, j, h][:, kb * 128:(kb + 1) * 128], a_sp[ai][:, h * 128:(h + 1) * 128],
                          ['a_sp%d' % ai], ['kscr'])
                kk = nxt('k')
                S.dma('pool', vblk[kk][:].rearrange("p a b -> p (a b)"), I['cache_v'][l, j, kb * 128:(kb + 1) * 128, :],
                      (), ['vblk%d' % kk])
                S.dma('sp', VSS[l, j, kb * 128:(kb + 1) * 128, :], vblk[kk][:].rearrange("p a b -> p (a b)"),
                      ['vblk%d' % kk], ['vscr'])

        prompt = dict(N=512, L=128, nsb=NSB_RUN, init=None, sidx=0, pos0=0, kbase=0,
                      xin=[I['xT'], X1], xout=[X1, O['yT']],
                      k_out=[[O['kT'][l, h] for h in range(H)] for l in range(DEPTH)],
                      v_out=[O['v'][l] for l in range(DEPTH)],
                      kscr=[[KSC[l, h] for h in range(H)] for l in range(DEPTH)],
                      vscr=[VSC[l] for l in range(DEPTH)])
        for l in range(NLAYER):
            run_seq(l, prompt)
        if DO_SAMPLE:
            for j in range(NS):
                sq_ = dict(N=TS, L=TS, nsb=1, init=j, sidx=1 + j, pos0=T, kbase=PAST,
                           xin=[I['xsT'][j], X1s[j]], xout=[X1s[j], O['ysT'][j]],
                           k_out=[[O['ksT'][l, j, h] for h in range(H)] for l in range(DEPTH)],
                           v_out=[O['vs'][l, j] for l in range(DEPTH)],
                           kscr=[[KSS[l, j, h] for h in range(H)] for l in range(DEPTH)],
                           vscr=[VSS[l, j] for l in range(DEPTH)])
                for l in range(NLAYER):
                    prep_sample(l, j)
                    run_seq(l, sq_)
        S.finish()
        S.emit()
        print("program ops:", S.nops, {e: len(S.prog[e]) for e in S.ENG})
    return nc


def _col(a):
    return np.ascontiguousarray(a.reshape(-1, 128).T)


def kernel(x_prompt, x_sample, cache_sb_k, cache_sb_v, state_mlstm_C, state_mlstm_n, state_mlstm_m,
           state_ret_S, state_conv, norm_g, w_in, mlstm_b_i, mlstm_b_f, conv_w, conv_b,
           conv_ln_g, conv_ln_b, w_branch, w_out, final_g):
    f = lambda a: np.ascontiguousarray(np.asarray(a, np.float32))
    x_prompt, x_sample = f(x_prompt), f(x_sample)
    cfa, cba = _build_consts()
    rc, rs = _rot_tables()
    nc = build_program()
    gcol = np.concatenate([_col(f(norm_g)[l]) for l in range(DEPTH)], 1)
    fgcol = _col(f(final_g))
    bif = np.tile(np.concatenate([np.concatenate([f(mlstm_b_i)[l], f(mlstm_b_f)[l]]) for l in range(DEPTH)])[None, :], (128, 1))
    cw = f(conv_w)
    convw = np.ascontiguousarray(cw.reshape(DEPTH, CW, 4, 128).transpose(3, 0, 2, 1).reshape(128, DEPTH * 4 * CW))
    c4 = lambda a: np.ascontiguousarray(f(a).reshape(DEPTH, 4, 128).transpose(2, 0, 1).reshape(128, DEPTH * 4))
    shared = dict(w_in=f(w_in), w_br=f(w_branch), w_out=f(w_out), gcol=gcol, fgcol=fgcol, bif=np.ascontiguousarray(bif),
                  convw=convw, convb=c4(conv_b), lng=c4(conv_ln_g), lnb=c4(conv_ln_b), rcos=rc, rsin=rs, cf=cfa, cb=cba)
    xTs = [np.ascontiguousarray(x_prompt[b].T) for b in range(2)]
    in_maps = []
    for c in range(8):
        sl = slice(NS * c, NS * (c + 1))
        m = dict(shared)
        m['xT'] = xTs[c % 2]
        m['xsT'] = np.ascontiguousarray(x_sample[sl].transpose(0, 2, 1))
        m['cache_k'] = np.ascontiguousarray(f(cache_sb_k)[:, sl].reshape(DEPTH, NS, PAST, 512))
        m['cache_v'] = np.ascontiguousarray(f(cache_sb_v)[:, sl].reshape(DEPTH, NS, PAST, 512))
        m['stC'] = np.ascontiguousarray(f(state_mlstm_C)[:, sl])
        m['stn'] = np.ascontiguousarray(f(state_mlstm_n)[:, sl].reshape(DEPTH * NS * H, 128).T)
        m['stm'] = np.ascontiguousarray(np.tile(f(state_mlstm_m)[:, sl].reshape(1, DEPTH * NS * H), (128, 1)))
        m['stS'] = np.ascontiguousarray(f(state_ret_S)[:, sl])
        m['stconv'] = np.ascontiguousarray(f(state_conv)[:, sl].reshape(DEPTH, NS, CW - 1, 4, 128).transpose(0, 1, 3, 4, 2))
        in_maps.append(m)
    ncores = int(os.environ.get('MK_CORES', '8'))
    if os.environ.get('MK_TRACE'):
        res = run_bass_kernel_spmd(nc, in_maps[:ncores], core_ids=list(range(ncores)), trace=True)
        print('EXEC_NS', res.exec_time_ns)
    else:
        res = run_bass_kernel_spmd(nc, in_maps[:ncores], core_ids=list(range(ncores)))
    R = res.results
    if os.environ.get('MK_RAW'):
        return R
    B = 2
    y_prompt = np.stack([R[b]['yT'].T for b in range(B)])
    y_sample = np.concatenate([R[c]['ysT'].transpose(0, 2, 1) for c in range(8)], 0)
    pk = np.stack([np.stack([R[b]['kT_o'][l].transpose(2, 0, 1) for b in range(B)]) for l in range(DEPTH)])
    pv = np.stack([np.stack([R[b]['v_o'][l].reshape(T, H, HD) for b in range(B)]) for l in range(DEPTH)])
    pC = np.stack([np.stack([R[b]['Cn_o'][l, 0, :, :, 0:128] for b in range(B)]) for l in range(DEPTH)])
    pn = np.stack([np.stack([R[b]['Cn_o'][l, 0, :, :, 128] for b in range(B)]) for l in range(DEPTH)])
    pm = np.stack([np.stack([R[b]['m_o'][l, 0, :, 0, 0] for b in range(B)]) for l in range(DEPTH)])
    pS = np.stack([np.stack([R[b]['S_o'][l, 0] for b in range(B)]) for l in range(DEPTH)])
    cvt = lambda a: a.transpose(2, 0, 1).reshape(CW - 1, 512)
    pconv = np.stack([np.stack([cvt(R[b]['conv_o'][l, 0]) for b in range(B)]) for l in range(DEPTH)])
    sk = np.stack([np.concatenate([R[c]['ksT_o'][l].transpose(0, 3, 1, 2) for c in range(8)], 0) for l in range(DEPTH)])
    sv = np.stack([np.concatenate([R[c]['vs_o'][l].reshape(NS, TS, H, HD) for c in range(8)], 0) for l in range(DEPTH)])
    sC = np.stack([np.concatenate([R[c]['Cn_o'][l, 1:, :, :, 0:128] for c in range(8)], 0) for l in range(DEPTH)])
    sn = np.stack([np.concatenate([R[c]['Cn_o'][l, 1:, :, :, 128] for c in range(8)], 0) for l in range(DEPTH)])
    sm_ = np.stack([np.concatenate([R[c]['m_o'][l, 1:, :, 0, 0] for c in range(8)], 0) for l in range(DEPTH)])
    sS = np.stack([np.concatenate([R[c]['S_o'][l, 1:] for c in range(8)], 0) for l in range(DEPTH)])
    sconv = np.stack([np.concatenate([np.stack([cvt(R[c]['conv_o'][l, 1 + j]) for j in range(NS)]) for c in range(8)], 0)
                      for l in range(DEPTH)])
    outs = (y_prompt, y_sample, pk, pv, pC, pn, pm, pS, pconv, sk, sv, sC, sn, sm_, sS, sconv)
    return tuple(np.ascontiguousarray(o, dtype=np.float32) for o in outs)
```
